# Optimizing a Trainium2 kernel written in Bass

```python
import math
import jax, jax.numpy as jnp
from jax import lax
import numpy as np

D_MODEL = 1024
BATCH = 8
SEQ = 4096
DEPTH = 1

D_RNN = 1280
RNN_BLOCK = 64
N_RNN_BLOCKS = D_RNN // RNN_BLOCK
CONV_WIDTH = 4
LRU_C = 8.0
N_HEADS_MLA = 16
QK_NOPE = 64
QK_ROPE = 32
V_HEAD = 64
Q_LORA = 384
KV_LORA = 256
ROPE_THETA = 10000.0
Q_BLOCK = 128
D_FF = 2816
FFN_CONV = 3
EPS = 1e-6
D_IN = D_RNN + Q_LORA + KV_LORA + QK_ROPE + 2 * D_MODEL

kernel_name = "hybrid_rglru_mla_convffn_adaln"


def rmsnorm(x, g):
    xf = x.astype(jnp.float32)
    y = xf * lax.rsqrt(jnp.mean(xf * xf, axis=-1, keepdims=True) + EPS)
    return (y * g.astype(jnp.float32)).astype(x.dtype)


def causal_dwconv(x, w, b):
    K = w.shape[0]
    S = x.shape[1]
    xp = jnp.pad(x, ((0, 0), (K - 1, 0), (0, 0)))
    y = xp[:, 0:S] * w[0]
    for k in range(1, K):
        y = y + xp[:, k:k + S] * w[k]
    return y + b


def rotary(x, positions):
    half = QK_ROPE // 2
    inv_freq = ROPE_THETA ** (-jnp.arange(half, dtype=jnp.float32) / half)
    ang = positions.astype(jnp.float32)[..., None] * inv_freq
    ang = ang.reshape(ang.shape[:2] + (1,) * (x.ndim - 3) + (half,))
    cos, sin = jnp.cos(ang), jnp.sin(ang)
    xf = x.astype(jnp.float32)
    x1, x2 = xf[..., :half], xf[..., half:]
    out = jnp.concatenate([x1 * cos - x2 * sin, x2 * cos + x1 * sin], axis=-1)
    return out.astype(x.dtype)


def rg_lru(xc, positions, w_a, b_a, w_x, b_x, lru_param):
    B, S, W = xc.shape
    xb = xc.reshape(B, S, N_RNN_BLOCKS, RNN_BLOCK)
    r = jax.nn.sigmoid(jnp.einsum('bsnd,nde->bsne', xb, w_a).reshape(B, S, W) + b_a)
    i = jax.nn.sigmoid(jnp.einsum('bsnd,nde->bsne', xb, w_x).reshape(B, S, W) + b_x)
    log_a = -LRU_C * r.astype(jnp.float32) * jax.nn.softplus(-lru_param.astype(jnp.float32))
    a = jnp.exp(log_a)
    mult = jnp.sqrt(-jnp.expm1(2.0 * log_a))
    reset = (positions == 0)[..., None]
    a = jnp.where(reset, 0.0, a)
    mult = jnp.where(reset, 1.0, mult)
    bterm = mult * (i * xc).astype(jnp.float32)

    def combine(lhs, rhs):
        a1, b1 = lhs
        a2, b2 = rhs
        return a1 * a2, a2 * b1 + b2

    _, h = lax.associative_scan(combine, (a, bterm), axis=1)
    return h.astype(xc.dtype)


def causal_block_attention(q, k, v):
    B, S, H, Dk = q.shape
    Dv = v.shape[-1]
    nb = S // Q_BLOCK
    scale = 1.0 / math.sqrt(Dk)
    qb = q.reshape(B, nb, Q_BLOCK, H, Dk).transpose(1, 0, 2, 3, 4)
    kpos = jnp.arange(S)

    def one_block(args):
        qi, blk = args
        s = jnp.einsum('bqhd,bkhd->bhqk', qi, k).astype(jnp.float32) * scale
        qpos = blk * Q_BLOCK + jnp.arange(Q_BLOCK)
        mask = kpos[None, :] <= qpos[:, None]
        s = jnp.where(mask[None, None], s, -jnp.inf)
        p = jax.nn.softmax(s, axis=-1).astype(v.dtype)
        return jnp.einsum('bhqk,bkhd->bqhd', p, v)

    o = lax.map(one_block, (qb, jnp.arange(nb)))
    return o.transpose(1, 0, 2, 3, 4).reshape(B, S, H * Dv)


def setup_inputs(seed: int = 0) -> dict:
    key = jax.random.key(seed)
    ks = jax.random.split(key, 32)

    def nrm(k, shape, fan_in, mult=1.0):
        return jax.random.normal(k, shape, jnp.float32) * (mult * fan_in ** -0.5)

    def gain(k, shape):
        return 1.0 + 0.02 * jax.random.normal(k, shape, jnp.float32)

    def bias(k, shape):
        return 0.01 * jax.random.normal(k, shape, jnp.float32)

    L = DEPTH
    a0 = jax.random.uniform(ks[10], (L, D_RNN), jnp.float32, 0.9, 0.999)
    s0 = a0 ** (1.0 / LRU_C)
    lru_param = jnp.log(s0) - jnp.log1p(-s0)
    positions = (jnp.arange(SEQ, dtype=jnp.int32)[None, :]
                 + jax.random.randint(ks[31], (BATCH, 1), 0, SEQ, dtype=jnp.int32))
    return {
        "x": jax.random.normal(ks[0], (BATCH, SEQ, D_MODEL), jnp.float32),
        "c": jax.random.normal(ks[1], (BATCH, D_MODEL), jnp.float32),
        "positions": positions,
        "w_ada": nrm(ks[2], (L, D_MODEL, 6 * D_MODEL), D_MODEL, 0.5),
        "b_ada": bias(ks[3], (L, 6 * D_MODEL)),
        "norm1_g": gain(ks[4], (L, D_MODEL)),
        "w_in": nrm(ks[5], (L, D_MODEL, D_IN), D_MODEL),
        "conv_w": nrm(ks[6], (L, CONV_WIDTH, D_RNN), CONV_WIDTH),
        "conv_b": bias(ks[7], (L, D_RNN)),
        "w_gate_a": nrm(ks[8], (L, N_RNN_BLOCKS, RNN_BLOCK, RNN_BLOCK), RNN_BLOCK),
        "b_gate_a": bias(ks[9], (L, D_RNN)),
        "w_gate_x": nrm(ks[11], (L, N_RNN_BLOCKS, RNN_BLOCK, RNN_BLOCK), RNN_BLOCK),
        "b_gate_x": bias(ks[12], (L, D_RNN)),
        "lru_param": lru_param,
        "q_norm_g": gain(ks[13], (L, Q_LORA)),
        "w_uq": nrm(ks[14], (L, Q_LORA, N_HEADS_MLA * (QK_NOPE + QK_ROPE)), Q_LORA),
        "kv_norm_g": gain(ks[15], (L, KV_LORA)),
        "w_ukv": nrm(ks[16], (L, KV_LORA, N_HEADS_MLA * (QK_NOPE + V_HEAD)), KV_LORA),
        "w_proj_rnn": nrm(ks[17], (L, D_RNN, D_MODEL), D_RNN),
        "w_proj_mla": nrm(ks[18], (L, N_HEADS_MLA * V_HEAD, D_MODEL), N_HEADS_MLA * V_HEAD),
        "w_out": nrm(ks[19], (L, D_MODEL, D_MODEL), D_MODEL),
        "norm2_g": gain(ks[20], (L, D_MODEL)),
        "w_up": nrm(ks[21], (L, D_MODEL, 2 * D_FF), D_MODEL),
        "ffn_conv_w": nrm(ks[22], (L, FFN_CONV, 2 * D_FF), FFN_CONV),
        "ffn_conv_b": bias(ks[23], (L, 2 * D_FF)),
        "w_down": nrm(ks[24], (L, D_FF, D_MODEL), D_FF),
        "final_g": gain(ks[25], (D_MODEL,)),
    }


def reference(x, c, positions, w_ada, b_ada, norm1_g, w_in, conv_w, conv_b,
              w_gate_a, b_gate_a, w_gate_x, b_gate_x, lru_param, q_norm_g, w_uq,
              kv_norm_g, w_ukv, w_proj_rnn, w_proj_mla, w_out, norm2_g, w_up,
              ffn_conv_w, ffn_conv_b, w_down, final_g):
    B, S, D = x.shape
    H = N_HEADS_MLA
    c_act = jax.nn.silu(c)
    for l in range(DEPTH):
        mod = c_act @ w_ada[l] + b_ada[l]
        shift1, scale1, gate1, shift2, scale2, gate2 = [m[:, None, :] for m in jnp.split(mod, 6, axis=-1)]

        h = rmsnorm(x, norm1_g[l]) * (1.0 + scale1) + shift1
        proj = h @ w_in[l]
        o0 = D_RNN
        o1 = o0 + Q_LORA
        o2 = o1 + KV_LORA
        o3 = o2 + QK_ROPE
        o4 = o3 + D_MODEL
        x_rnn, q_lat, kv_lat, k_rope, g_rnn, g_mla = (
            proj[..., :o0], proj[..., o0:o1], proj[..., o1:o2],
            proj[..., o2:o3], proj[..., o3:o4], proj[..., o4:])

        xc = causal_dwconv(x_rnn, conv_w[l], conv_b[l])
        y_rnn = rg_lru(xc, positions, w_gate_a[l], b_gate_a[l], w_gate_x[l], b_gate_x[l], lru_param[l])

        q = (rmsnorm(q_lat, q_norm_g[l]) @ w_uq[l]).reshape(B, S, H, QK_NOPE + QK_ROPE)
        q_rope = rotary(q[..., QK_NOPE:], positions)
        q = jnp.concatenate([q[..., :QK_NOPE], q_rope], axis=-1)
        kv = (rmsnorm(kv_lat, kv_norm_g[l]) @ w_ukv[l]).reshape(B, S, H, QK_NOPE + V_HEAD)
        k_nope, v = kv[..., :QK_NOPE], kv[..., QK_NOPE:]
        k_r = jnp.broadcast_to(rotary(k_rope, positions)[:, :, None, :], (B, S, H, QK_ROPE))
        k = jnp.concatenate([k_nope, k_r], axis=-1)
        y_mla = causal_block_attention(q, k, v)

        merged = (jax.nn.sigmoid(g_rnn) * (y_rnn @ w_proj_rnn[l])
                  + jax.nn.sigmoid(g_mla) * (y_mla @ w_proj_mla[l]))
        x = x + gate1 * (merged @ w_out[l])

        h2 = rmsnorm(x, norm2_g[l]) * (1.0 + scale2) + shift2
        u = causal_dwconv(h2 @ w_up[l], ffn_conv_w[l], ffn_conv_b[l])
        u_gate, u_val = u[..., :D_FF], u[..., D_FF:]
        x = x + gate2 * ((jax.nn.silu(u_gate) * u_val) @ w_down[l])

    return rmsnorm(x, final_g)
```

```python
import math
from contextlib import ExitStack

import numpy as np
import concourse.bass as bass
import concourse.mybir as mybir
from concourse.bass_utils import run_bass_kernel_spmd

F32 = mybir.dt.float32
BF16 = mybir.dt.bfloat16
I32 = mybir.dt.int32
AF = mybir.ActivationFunctionType
ALU = mybir.AluOpType

S_LEN = 4096
D = 1024
D_RNN = 1280
NH = 16
D_FF = 2816
EPS = 1e-6
NP_IN = 4160
C_RNN0, C_Q0, C_KV0, C_RM0, C_RR0, C_GR0, C_GM0 = 0, 1280, 1664, 1920, 2016, 2112, 3136
TWO_PI = 2.0 * math.pi
CW1 = 6.28125
CW2 = TWO_PI - CW1
PI_SAFE = 3.141592


class Sched:
    ENG = ('pe', 'act', 'dve', 'pool', 'sp')

    def __init__(self, nc, es):
        self.nc = nc
        self.es = es
        self.streams = {e: [] for e in self.ENG}
        self.esem = {e: es.enter_context(nc.semaphore("prog_" + e)) for e in ('pe', 'act', 'dve', 'pool')}
        self.ecnt = {e: 0 for e in self.esem}
        self.known = {e: {} for e in self.ENG}
        self.lastw = {}
        self.rd = {}
        self.dsem = {}
        self.dcnt = {}
        self.nops = 0

    def _dma_sem(self, key):
        if key not in self.dsem:
            self.dsem[key] = self.es.enter_context(self.nc.semaphore("dma_" + str(key)))
            self.dcnt[key] = 0
        return self.dsem[key]

    def _resolve(self, eng, toks):
        need = {}
        for (kind, key, val) in toks:
            if kind == 'e':
                if key == eng and eng == 'pe':
                    continue
                k = ('e', key)
                v = val
            else:
                k = ('d', key)
                v = self.dcnt[key]
            if need.get(k, 0) < v:
                need[k] = v
        out = []
        for k, v in need.items():
            if self.known[eng].get(k, 0) >= v:
                continue
            self.known[eng][k] = v
            sem = self.esem[k[1]] if k[0] == 'e' else self.dsem[k[1]]
            out.append((sem, v))
        return out

    def _deps(self, eng, reads, writes):
        toks = []
        for r in reads:
            if r in self.lastw:
                toks.append(self.lastw[r])
        for w in writes:
            if w in self.lastw:
                toks.append(self.lastw[w])
            toks.extend(self.rd.get(w, ()))
        return self._resolve(eng, toks)

    def _commit(self, tok, reads, writes):
        for r in reads:
            self.rd.setdefault(r, []).append(tok)
        for w in writes:
            self.lastw[w] = tok
            self.rd[w] = []

    def op(self, eng, fn, reads=(), writes=(), inc=True):
        rec = _Rec()
        fn(rec)
        assert len(rec.calls) == 1
        mname, margs, mkw = rec.calls[0]

        def fn(e, mname=mname, margs=margs, mkw=mkw):
            return getattr(e, mname)(*margs, **mkw)
        waits = self._deps(eng, reads, writes)
        self.nops += 1
        sem = self.esem[eng]
        if inc:
            self.ecnt[eng] += 1

        def run(e, waits=waits, fn=fn, sem=sem, inc=inc):
            for (s, v) in waits:
                e.wait_ge(s, v)
            ins = fn(e)
            if inc:
                ins.then_inc(sem, 1)
        self.streams[eng].append(run)
        self._commit(('e', eng, self.ecnt[eng] if inc else self.ecnt[eng] + 1), reads, writes)

    def dma(self, out, in_, semkey, reads=(), writes=(), q='sp'):
        waits = self._deps(q, reads, writes)
        sem = self._dma_sem(semkey)
        self.dcnt[semkey] += 16
        self.nops += 1

        def run(e, waits=waits, sem=sem, out=out, in_=in_):
            for (s, v) in waits:
                e.wait_ge(s, v)
            e.dma_start(out=out, in_=in_).then_inc(sem, 16)
        self.streams[q].append(run)
        self._commit(('d', semkey, None), reads, writes)

    def barrier(self):
        toks = [('e', e, self.ecnt[e]) for e in self.esem if self.ecnt[e] > 0]
        toks += [('d', k, None) for k in self.dsem if self.dcnt[k] > 0]
        for eng in self.ENG:
            waits = self._resolve(eng, [t for t in toks if not (t[0] == 'e' and t[1] == eng)])

            def run(e, waits=waits):
                for (s, v) in waits:
                    e.wait_ge(s, v)
            self.streams[eng].append(run)

    def emit(self):
        nc = self.nc
        st = self.streams
        with nc.Block() as block:
            @block.tensor
            def _(e):
                for f in st['pe']:
                    f(e)

            @block.scalar
            def _(e):
                for f in st['act']:
                    f(e)

            @block.vector
            def _(e):
                for f in st['dve']:
                    f(e)

            @block.gpsimd
            def _(e):
                for f in st['pool']:
                    f(e)

            @block.sync
            def _(e):
                for f in st['sp']:
                    f(e)


class _Rec:
    def __init__(self):
        self.calls = []

    def __getattr__(self, name):
        def f(*a, **kw):
            self.calls.append((name, a, kw))
        return f


class Ring:
    def __init__(self, items):
        self.items = list(items)
        self.i = 0

    def next(self):
        it = self.items[self.i % len(self.items)]
        self.i += 1
        return it


def bcast_rows(ap2d, nparts):
    n = ap2d.shape[-1]
    return bass.AP(ap2d.tensor, ap2d.offset, [[0, nparts], [1, n]])


def build_program(upto=99, dbg=False):
    nc = bass.Bass("TRN2", target_bir_lowering=False)
    dbg_outs = {}

    def din(name, shape, dt=F32):
        return nc.dram_tensor(name, shape, dt, kind="ExternalInput").ap()

    x_d = din("x", [S_LEN, D])
    c_d = din("c_col", [128, 8])
    pos_d = din("pos", [1, S_LEN], I32)
    wada_d = din("w_ada", [128, 8, 6144])
    rows_d = din("rows", [1, 6144 + 3 * 1024])
    cols_d = din("cols", [128, 128])
    win_d = din("w_in_p", [128, 8, NP_IN])
    wga_d = din("w_ga", [128, 10, 128])
    wgx_d = din("w_gx", [128, 10, 128])
    wuqm_d = din("w_uq_m", [128, 3, 1536])
    wuqr_d = din("w_uq_r", [128, 3, 1536])
    wk_d = din("w_k", [128, 2, 1024])
    wv_d = din("w_v", [128, 2, 1024])
    wpr_d = din("w_pr", [128, 10, 1024])
    wpm_d = din("w_pm", [128, 8, 1024])
    wo_d = din("w_o", [128, 8, 1024])
    wup_d = din("w_up", [128, 8, 2 * D_FF])
    wdn_d = din("w_dn", [128, 22, 1024])
    fcw_d = din("fcw", [128, 44 * 3])
    fcb_d = din("fcb", [128, 44])
    ident_d = din("ident", [128, 128])
    out_d = nc.dram_tensor("out", [S_LEN, D], F32, kind="ExternalOutput").ap()

    mod_s = nc.dram_tensor("mod_s", [1, 6144], F32).ap()
    mrnn_s = nc.dram_tensor("mrnn_s", [D, S_LEN], BF16).ap()
    sgm_s = nc.dram_tensor("sgm_s", [D, S_LEN], BF16).ap()
    qn_s = nc.dram_tensor("qn_s", [384, S_LEN], BF16).ap()
    kvn_s = nc.dram_tensor("kvn_s", [256, S_LEN], BF16).ap()
    kr_s = nc.dram_tensor("kr_s", [128, S_LEN], BF16).ap()
    ymla_s = nc.dram_tensor("ymla_s", [S_LEN, D], BF16).ap()
    x1_s = nc.dram_tensor("x1_s", [S_LEN, D], F32).ap()

    CO_CONVW, CO_CONVB, CO_BA, CO_BX, CO_LRU, CO_GQ, CO_GKV, CO_INVF = 0, 40, 50, 60, 70, 80, 83, 85

    with ExitStack() as es:
        S = Sched(nc, es)

        def sbt(st, name, shape, dt):
            return st.enter_context(nc.sbuf_tensor("s_" + name, shape, dt))

        pb = [es.enter_context(nc.psum_tensor("pb%d" % i, [128, 512], F32)) for i in range(8)]
        PB = lambda i: ('pb', i)

        def dbg_out(name, ap_sb, shape, dt, res):
            if not dbg:
                return
            t = nc.dram_tensor("dbg_" + name, list(shape), dt, kind="ExternalOutput").ap()
            S.dma(t, ap_sb, 'dbg_' + name, reads=res, writes=['dbgd_' + name])
            dbg_outs[name] = t

        ident = sbt(es, "ident", [128, 128], F32)
        identb = sbt(es, "identb", [128, 128], BF16)
        cols = sbt(es, "cols", [128, 128], F32)
        S.dma(ident[:], ident_d, 'c0', writes=['ident'])
        S.dma(cols[:], cols_d, 'c0', writes=['cols'])
        S.op('dve', lambda e: e.tensor_copy(out=identb[:], in_=ident[:]), reads=['ident'], writes=['identb'])

        stg_ctr = [0]

        def load_weights(items):
            PIECE = 6144
            with ExitStack() as st:
                stg_ctr[0] += 1
                stg = [sbt(st, "stg%d_%d" % (stg_ctr[0], i), [128, PIECE], F32) for i in range(2)]
                engs = Ring(['act', 'pool', 'dve', 'act'])
                cnt = 0
                for d2, s2, res in items:
                    Fd = d2.shape[1]
                    for o in range(0, Fd, PIECE):
                        n = min(PIECE, Fd - o)
                        si = cnt % 2
                        cnt += 1
                        S.dma(stg[si][:, 0:n], s2[:, o:o + n], 'stg%d' % si, writes=['stg%d' % si])
                        eng = engs.next()
                        if eng == 'act':
                            S.op(eng, lambda e: e.copy(out=d2[:, o:o + n], in_=stg[si][:, 0:n]), reads=['stg%d' % si], writes=[res])
                        else:
                            S.op(eng, lambda e: e.tensor_copy(out=d2[:, o:o + n], in_=stg[si][:, 0:n]), reads=['stg%d' % si], writes=[res])
                S.barrier()

        def f2(ap):
            return ap.rearrange("p k n -> p (k n)")

        def make_cs_runner(st, n, tag):
            posi = sbt(st, tag + "_posi", [96, n], I32)
            ang = sbt(st, tag + "_ang", [96, n], F32)
            kf = sbt(st, tag + "_kf", [96, n], F32)
            ki = sbt(st, tag + "_ki", [96, n], I32)
            R = lambda s_: tag + s_

            def wrap():
                S.op('dve', lambda e: e.tensor_scalar(out=kf[:], in0=ang[:], scalar1=math.pi, scalar2=-TWO_PI, op0=ALU.is_gt, op1=ALU.mult),
                     reads=[R('ang')], writes=[R('kf')])
                S.op('dve', lambda e: e.tensor_tensor(out=ang[:], in0=ang[:], in1=kf[:], op=ALU.add), reads=[R('ang'), R('kf')], writes=[R('ang')])
                S.op('dve', lambda e: e.tensor_scalar(out=ang[:], in0=ang[:], scalar1=PI_SAFE, scalar2=-PI_SAFE, op0=ALU.min, op1=ALU.max),
                     reads=[R('ang')], writes=[R('ang')])

            def run(tok0, cos_ap, sin_ap, cosres, sinres):
                S.dma(posi[:], bcast_rows(pos_d[0:1, tok0:tok0 + n], 96), 'pos_' + tag, writes=[R('posi')])
                S.op('dve', lambda e: e.tensor_copy(out=ang[:], in_=posi[:]), reads=[R('posi')], writes=[R('ang')])
                S.op('dve', lambda e: e.tensor_scalar(out=ang[:], in0=ang[:], scalar1=cols[0:96, CO_INVF:CO_INVF + 1], scalar2=None,
                                                      op0=ALU.mult), reads=[R('ang'), 'cols'], writes=[R('ang')])
                S.op('dve', lambda e: e.tensor_scalar(out=kf[:], in0=ang[:], scalar1=1.0 / TWO_PI, scalar2=None, op0=ALU.mult),
                     reads=[R('ang')], writes=[R('kf')])
                S.op('dve', lambda e: e.tensor_copy(out=ki[:], in_=kf[:]), reads=[R('kf')], writes=[R('ki')])
                S.op('dve', lambda e: e.tensor_copy(out=kf[:], in_=ki[:]), reads=[R('ki')], writes=[R('kf')])
                S.op('dve', lambda e: e.scalar_tensor_tensor(out=ang[:], in0=kf[:], scalar=-CW1, in1=ang[:], op0=ALU.mult, op1=ALU.add),
                     reads=[R('kf'), R('ang')], writes=[R('ang')])
                S.op('dve', lambda e: e.scalar_tensor_tensor(out=ang[:], in0=kf[:], scalar=-CW2, in1=ang[:], op0=ALU.mult, op1=ALU.add),
                     reads=[R('kf'), R('ang')], writes=[R('ang')])
                wrap()
                S.op('act', lambda e: e.activation(out=sin_ap, in_=ang[:], func=AF.Sin), reads=[R('ang')], writes=[sinres])
                S.op('dve', lambda e: e.tensor_scalar(out=ang[:], in0=ang[:], scalar1=math.pi / 2, scalar2=None, op0=ALU.add),
                     reads=[R('ang'), sinres], writes=[R('ang')])
                wrap()
                S.op('act', lambda e: e.activation(out=cos_ap, in_=ang[:], func=AF.Sin), reads=[R('ang')], writes=[cosres])
            return run

        def load_bc(st, name, col0, n=1024):
            t = sbt(st, name, [128, n], F32)
            S.dma(t[:], bcast_rows(mod_s[0:1, col0:col0 + n], 128), 'bc_' + name, reads=['mod_s'], writes=[name])
            return t

        def load_row_bc(st, name, col0, n=1024):
            t = sbt(st, name, [128, n], F32)
            S.dma(t[:], bcast_rows(rows_d[0:1, col0:col0 + n], 128), 'bc_' + name, writes=[name])
            return t

        with ExitStack() as ph:
            ccol = sbt(ph, "ccol", [128, 8], F32)
            cact = sbt(ph, "cact", [128, 8], F32)
            brow = sbt(ph, "brow", [1, 6144], F32)
            mrow = sbt(ph, "mrow", [1, 6144], F32)
            wa = [sbt(ph, "wa%d" % i, [128, 8, 1024], F32) for i in range(2)]
            S.dma(ccol[:], c_d, 'p0a', writes=['ccol'])
            S.dma(brow[:], rows_d[0:1, 0:6144], 'p0a', writes=['brow'])
            S.op('act', lambda e: e.activation(out=cact[:], in_=ccol[:], func=AF.Silu), reads=['ccol'], writes=['cact'])
            for piece in range(6):
                w = wa[piece % 2]
                wres = 'wa%d' % (piece % 2)
                S.dma(w[:], wada_d[:, :, piece * 1024:(piece + 1) * 1024], wres, writes=[wres])
                for half in range(2):
                    bank = piece * 2 + half
                    b = pb[bank % 4]
                    for k in range(8):
                        S.op('pe', lambda e, b=b, w=w, k=k, half=half: e.matmul(
                            b[0:1, :], lhsT=cact[:, k:k + 1], rhs=w[:, k, half * 512:(half + 1) * 512], start=(k == 0), stop=(k == 7)),
                            reads=['cact', wres], writes=[PB(bank % 4)], inc=(k == 7))
                    c0 = bank * 512
                    S.op('dve', lambda e, b=b, c0=c0: e.tensor_tensor(out=mrow[0:1, c0:c0 + 512], in0=b[0:1, :], in1=brow[0:1, c0:c0 + 512], op=ALU.add),
                         reads=[PB(bank % 4), 'brow'], writes=['mrow'])
            S.dma(mod_s, mrow[:], 'mods', reads=['mrow'], writes=['mod_s'])
            dbg_out("mod", mrow[:], [1, 6144], F32, ['mrow'])
            S.barrier()
        if upto <= 0:
            S.barrier()
            S.emit()
            return nc, dbg_outs

        def make_gm(st, name, scale_col0, g_col0, gbuf, gres):
            gm = load_bc(st, name, scale_col0)
            S.dma(gbuf[:], bcast_rows(rows_d[0:1, g_col0:g_col0 + 1024], 128), 'bc_g_' + name, writes=[gres])
            S.op('dve', lambda e: e.scalar_tensor_tensor(out=gm[:], in0=gm[:], scalar=1.0, in1=gbuf[:], op0=ALU.add, op1=ALU.mult),
                 reads=[name, gres], writes=[name])
            return gm

        def norm_to_featmajor(xin_ap, xres, gm, gmres, sh, shres, tmp, tmpres, junk, junkres, hb, hbres, ss, ssres, pbank, dst, dstres):
            S.op('act', lambda e: e.activation(out=junk[:], in_=xin_ap, func=AF.Square, accum_out=ss[:, 0:1]),
                 reads=[xres], writes=[junkres, ssres])
            S.op('act', lambda e: e.activation(out=ss[:, 1:2], in_=ss[:, 0:1], func=AF.Sqrt, scale=1.0 / D, bias=cols[:, 127:128]),
                 reads=[ssres, 'cols'], writes=[ssres])
            S.op('dve', lambda e: e.reciprocal(out=ss[:, 2:3], in_=ss[:, 1:2]), reads=[ssres], writes=[ssres])
            S.op('dve', lambda e: e.scalar_tensor_tensor(out=tmp[:], in0=xin_ap, scalar=ss[:, 2:3], in1=gm[:], op0=ALU.mult, op1=ALU.mult),
                 reads=[xres, ssres, gmres], writes=[tmpres])
            S.op('pool', lambda e: e.tensor_tensor(out=hb[:], in0=tmp[:], in1=sh[:], op=ALU.add),
                 reads=[tmpres, shres], writes=[hbres])
            pbv = pb[pbank][:].bitcast(BF16)
            for k in range(8):
                S.op('pe', lambda e, k=k: e.transpose(out=pbv[:, k * 128:(k + 1) * 128], in_=hb[:, k * 128:(k + 1) * 128], identity=identb[:]),
                     reads=[hbres, 'identb'], writes=[PB(pbank)], inc=(k == 7))
            S.op('act', lambda e: e.copy(out=dst, in_=pbv.rearrange("p (k t) -> p k t", k=8)), reads=[PB(pbank)], writes=[dstres])

        T = 256
        NT = S_LEN // T
        NBLK = T // 128

        with ExitStack() as ph:
            win = sbt(ph, "win", [128, 8, NP_IN], BF16)
            wga = sbt(ph, "wga", [128, 10, 128], BF16)
            wgx = sbt(ph, "wgx", [128, 10, 128], BF16)
            wpr = sbt(ph, "wpr", [128, 10, 1024], BF16)
            load_weights([(f2(win[:]), f2(win_d), 'win'), (f2(wga[:]), f2(wga_d), 'wga'), (f2(wgx[:]), f2(wgx_d), 'wgx'),
                          (f2(wpr[:]), f2(wpr_d), 'wpr')])
            S.op('dve', lambda e: e.tensor_scalar(out=win[:, :, C_RR0 + 64:C_RR0 + 80], in0=win[:, :, C_RR0 + 64:C_RR0 + 80],
                                                  scalar1=-1.0, scalar2=None, op0=ALU.mult), reads=['win'], writes=['win'])
            tmpn = sbt(ph, "tmpn", [128, 1024], F32)
            gm1 = make_gm(ph, "gm1", 1024, 6144, tmpn, "tmpn")
            sh1 = load_bc(ph, "sh1", 0)
            onesq = sbt(ph, "onesq", [128, 128], F32)
            oneskv = sbt(ph, "oneskv", [128, 128], F32)
            S.op('pool', lambda e: e.memset(onesq[:], 1.0 / 384), writes=['onesq'])
            S.op('pool', lambda e: e.memset(oneskv[:], 1.0 / 256), writes=['oneskv'])
            cL = sbt(ph, "cL", [128, 10], F32)
            S.op('act', lambda e: e.activation(out=cL[:], in_=cols[:, CO_LRU:CO_LRU + 10], func=AF.Sigmoid), reads=['cols'], writes=['cL'])
            S.op('act', lambda e: e.activation(out=cL[:], in_=cL[:], func=AF.Ln), reads=['cL'], writes=['cL'])
            S.op('dve', lambda e: e.tensor_scalar(out=cL[:], in0=cL[:], scalar1=8.0, scalar2=None, op0=ALU.mult), reads=['cL'], writes=['cL'])
            cL2 = sbt(ph, "cL2", [128, 10], F32)
            S.op('dve', lambda e: e.tensor_scalar(out=cL2[:], in0=cL[:], scalar1=2.0, scalar2=None, op0=ALU.mult), reads=['cL'], writes=['cL2'])

            xin = [sbt(ph, "xin%d" % i, [128, 1024], F32) for i in range(2)]
            xring = Ring(range(2))
            junk = sbt(ph, "junk", [128, 1024], BF16)
            hb = [sbt(ph, "hb%d" % i, [128, 1024], BF16) for i in range(2)]
            ssn = [sbt(ph, "ssn%d" % i, [128, 4], F32) for i in range(2)]
            hT = [sbt(ph, "hT%d" % i, [128, 8, T], BF16) for i in range(2)]
            carry = sbt(ph, "carry", [128, 10, 4], F32)
            hcar = sbt(ph, "hcar", [128, 10], F32)
            S.op('pool', lambda e: e.memset(carry[:], 0.0), writes=['carry'])
            S.op('pool', lambda e: e.memset(hcar[:], 0.0), writes=['hcar'])
            xs = [sbt(ph, "xs%d" % i, [128, T + 4], F32) for i in range(2)]
            NB = 2
            xc = [sbt(ph, "xc%d" % i, [128, T], F32) for i in range(NB)]
            xcb = [sbt(ph, "xcb%d" % i, [128, T], BF16) for i in range(NB)]
            rr = [sbt(ph, "rr%d" % i, [128, T], F32) for i in range(NB)]
            ig = [sbt(ph, "ig%d" % i, [128, T], F32) for i in range(NB)]
            aa = [sbt(ph, "aa%d" % i, [128, T], F32) for i in range(NB)]
            mu = [sbt(ph, "mu%d" % i, [128, T], F32) for i in range(NB)]
            bt = [sbt(ph, "bt%d" % i, [128, T], F32) for i in range(NB)]
            hs = [sbt(ph, "hs%d" % i, [128, T], F32) for i in range(NB)]
            yb = sbt(ph, "yb", [128, 10, T], BF16)
            posb = sbt(ph, "posb", [128, T], I32)
            notm = sbt(ph, "notm", [128, T], F32)
            sg = [sbt(ph, "sg%d" % i, [128, T], F32) for i in range(2)]
            mrt = sbt(ph, "mrt", [128, 8, T], BF16)
            sgt = sbt(ph, "sgt", [128, 8, T], BF16)
            ql = sbt(ph, "ql", [128, 3, T], F32)
            sq = sbt(ph, "sq", [128, 3, T], F32)
            rs = sbt(ph, "rs", [128, T], F32)
            qnt = sbt(ph, "qnt", [128, 3, T], BF16)
            kvnt = sbt(ph, "kvnt", [128, 2, T], BF16)
            krt = sbt(ph, "krt", [128, T], BF16)
            cos_t = sbt(ph, "cos_t", [96, T], F32)
            sin_t = sbt(ph, "sin_t", [96, T], F32)
            t1 = sbt(ph, "t1", [96, T], F32)
            t2 = sbt(ph, "t2", [96, T], F32)
            S.op('pool', lambda e: e.memset(krt[:], 0.0), writes=['krt'])

            mmring = Ring([2, 3, 4, 5, 6, 7])

            def proj_mm(bank, col0, M, hres, hTt, ncols=T):
                for k in range(8):
                    S.op('pe', lambda e, k=k: e.matmul(pb[bank][0:M, 0:ncols], lhsT=win[:, k, col0:col0 + M], rhs=hTt[:, k, :],
                                                        start=(k == 0), stop=(k == 7)),
                         reads=['win', hres], writes=[PB(bank)], inc=(k == 7))

            def phase1_tile(i):
                tok0 = i * T
                slot = i % 2
                hTt = hT[slot]
                hres = 'hT%d' % slot
                for blk in range(NBLK):
                    xi = xring.next()
                    r0 = tok0 + blk * 128
                    S.dma(xin[xi][:], x_d[r0:r0 + 128, :], 'xin%d' % xi, writes=['xin%d' % xi])
                    nb = blk % 2
                    norm_to_featmajor(xin[xi][:], 'xin%d' % xi, gm1, 'gm1', sh1, 'sh1', tmpn, 'tmpn', junk, 'junk', hb[nb], 'hb%d' % nb,
                                      ssn[nb], 'ssn%d' % nb, nb, hTt[:, :, blk * 128:(blk + 1) * 128], hres)
                S.dma(posb[:], bcast_rows(pos_d[0:1, tok0:tok0 + T], 128), 'posb', writes=['posb'])
                S.op('dve', lambda e: e.tensor_copy(out=notm[:], in_=posb[:]), reads=['posb'], writes=['notm'])
                S.op('dve', lambda e: e.tensor_scalar(out=notm[:], in0=notm[:], scalar1=0.0, scalar2=None, op0=ALU.not_equal),
                     reads=['notm'], writes=['notm'])
                for c in range(10):
                    bank = mmring.next()
                    proj_mm(bank, C_RNN0 + c * 128, 128, hres, hTt)
                    s = c % 2
                    n = c % NB
                    xsr = 'xs%d' % s
                    S.op('pool', lambda e, c=c, s=s: e.tensor_copy(out=xs[s][:, 1:4], in_=carry[:, c, 1:4]), reads=['carry'], writes=[xsr])
                    S.op('act', lambda e, s=s, bank=bank: e.copy(out=xs[s][:, 4:T + 4], in_=pb[bank][:, 0:T]), reads=[PB(bank)], writes=[xsr])
                    S.op('pool', lambda e, c=c, s=s: e.tensor_copy(out=carry[:, c, 1:4], in_=xs[s][:, T + 1:T + 4]), reads=[xsr], writes=['carry'])
                    cw = lambda k, c=c: cols[:, CO_CONVW + c * 4 + k:CO_CONVW + c * 4 + k + 1]
                    S.op('dve', lambda e, c=c, s=s, n=n, cw=cw: e.tensor_scalar(out=xc[n][:], in0=xs[s][:, 1:T + 1], scalar1=cw(0), scalar2=cols[:, CO_CONVB + c:CO_CONVB + c + 1],
                                                                          op0=ALU.mult, op1=ALU.add), reads=[xsr, 'cols'], writes=['xc%d' % n])
                    for k in range(1, 4):
                        S.op('dve', lambda e, k=k, s=s, n=n, cw=cw: e.scalar_tensor_tensor(out=xc[n][:], in0=xs[s][:, 1 + k:T + 1 + k], scalar=cw(k), in1=xc[n][:],
                                                                                     op0=ALU.mult, op1=ALU.add), reads=[xsr, 'cols', 'xc%d' % n], writes=['xc%d' % n])
                    S.op('pool', lambda e, n=n: e.tensor_copy(out=xcb[n][:], in_=xc[n][:]), reads=['xc%d' % n], writes=['xcb%d' % n])
                    ba_bank = mmring.next()
                    S.op('pe', lambda e, c=c, n=n, b=ba_bank: e.matmul(pb[b][:, 0:T], lhsT=wga[:, c, :], rhs=xcb[n][:], start=True, stop=True),
                         reads=['wga', 'xcb%d' % n], writes=[PB(ba_bank)])
                    bx_bank = mmring.next()
                    S.op('pe', lambda e, c=c, n=n, b=bx_bank: e.matmul(pb[b][:, 0:T], lhsT=wgx[:, c, :], rhs=xcb[n][:], start=True, stop=True),
                         reads=['wgx', 'xcb%d' % n], writes=[PB(bx_bank)])
                    S.op('act', lambda e, c=c, n=n, b=ba_bank: e.activation(out=rr[n][:], in_=pb[b][:, 0:T], func=AF.Sigmoid, bias=cols[:, CO_BA + c:CO_BA + c + 1]),
                         reads=[PB(ba_bank), 'cols'], writes=['rr%d' % n])
                    S.op('act', lambda e, c=c, n=n, b=bx_bank: e.activation(out=ig[n][:], in_=pb[b][:, 0:T], func=AF.Sigmoid, bias=cols[:, CO_BX + c:CO_BX + c + 1]),
                         reads=[PB(bx_bank), 'cols'], writes=['ig%d' % n])
                    S.op('act', lambda e, c=c, n=n: e.activation(out=aa[n][:], in_=rr[n][:], func=AF.Exp, scale=cL[:, c:c + 1]),
                         reads=['rr%d' % n, 'cL'], writes=['aa%d' % n])
                    S.op('dve', lambda e, n=n: e.tensor_tensor(out=aa[n][:], in0=aa[n][:], in1=notm[:], op=ALU.mult),
                         reads=['aa%d' % n, 'notm'], writes=['aa%d' % n])
                    S.op('act', lambda e, n=n: e.activation(out=mu[n][:], in_=aa[n][:], func=AF.Square), reads=['aa%d' % n], writes=['mu%d' % n])
                    S.op('act', lambda e, n=n: e.activation(out=mu[n][:], in_=mu[n][:], func=AF.Sqrt, scale=-1.0, bias=cols[:, 126:127]),
                         reads=['mu%d' % n, 'cols'], writes=['mu%d' % n])
                    S.op('pool', lambda e, n=n: e.tensor_tensor(out=bt[n][:], in0=ig[n][:], in1=xc[n][:], op=ALU.mult),
                         reads=['ig%d' % n, 'xc%d' % n], writes=['bt%d' % n])
                    S.op('dve', lambda e, n=n: e.tensor_tensor(out=bt[n][:], in0=bt[n][:], in1=mu[n][:], op=ALU.mult),
                         reads=['bt%d' % n, 'mu%d' % n], writes=['bt%d' % n])
                    S.op('dve', lambda e, c=c, n=n: e.tensor_tensor_scan(out=hs[n][:], data0=aa[n][:], data1=bt[n][:], initial=hcar[:, c:c + 1],
                                                                          op0=ALU.mult, op1=ALU.add),
                         reads=['aa%d' % n, 'bt%d' % n, 'hcar'], writes=['hs%d' % n])
                    S.op('dve', lambda e, c=c, n=n: e.tensor_copy(out=hcar[:, c:c + 1], in_=hs[n][:, T - 1:T]), reads=['hs%d' % n], writes=['hcar'])
                    S.op('pool', lambda e, c=c, n=n: e.tensor_copy(out=yb[:, c, :], in_=hs[n][:]), reads=['hs%d' % n], writes=[('yb', c)])
                    if dbg and upto <= 1 and i == 1 and c == 0:
                        dbg_out("xc", xc[n][:], [128, T], F32, ['xc%d' % n])
                        dbg_out("hs", hs[n][:], [128, T], F32, ['hs%d' % n])
                        dbg_out("aa", aa[n][:], [128, T], F32, ['aa%d' % n])
                for oc in range(8):
                    gb = mmring.next()
                    proj_mm(gb, C_GR0 + oc * 128, 128, hres, hTt)
                    s = oc % 2
                    S.op('act', lambda e, gb=gb, s=s: e.activation(out=sg[s][:], in_=pb[gb][:, 0:T], func=AF.Sigmoid), reads=[PB(gb)], writes=['sg%d' % s])
                    pbk = mmring.next()
                    for c in range(10):
                        S.op('pe', lambda e, c=c, oc=oc, pbk=pbk: e.matmul(pb[pbk][:, 0:T], lhsT=wpr[:, c, oc * 128:(oc + 1) * 128], rhs=yb[:, c, :],
                                                                          start=(c == 0), stop=(c == 9)),
                             reads=['wpr', ('yb', c)], writes=[PB(pbk)], inc=(c == 9))
                    S.op('dve', lambda e, oc=oc, s=s, pbk=pbk: e.tensor_tensor(out=mrt[:, oc, :], in0=pb[pbk][:, 0:T], in1=sg[s][:], op=ALU.mult),
                         reads=[PB(pbk), 'sg%d' % s], writes=['mrt'])
                S.dma(mrnn_s.rearrange("(o p) t -> p o t", p=128)[:, :, tok0:tok0 + T], mrt[:], 'mrt', reads=['mrt'], writes=['mrnn_s'])
                for oc in range(8):
                    gb = mmring.next()
                    proj_mm(gb, C_GM0 + oc * 128, 128, hres, hTt)
                    S.op('act', lambda e, gb=gb, oc=oc: e.activation(out=sgt[:, oc, :], in_=pb[gb][:, 0:T], func=AF.Sigmoid), reads=[PB(gb)], writes=['sgt'])
                S.dma(sgm_s.rearrange("(o p) t -> p o t", p=128)[:, :, tok0:tok0 + T], sgt[:], 'sgt', reads=['sgt'], writes=['sgm_s'])

                def latent(nch, col0, ones_t, onesres, gco, outt, outres, dram, semk):
                    for qc in range(nch):
                        b = mmring.next()
                        proj_mm(b, col0 + qc * 128, 128, hres, hTt)
                        S.op('act', lambda e, b=b, qc=qc: e.copy(out=ql[:, qc, :], in_=pb[b][:, 0:T]), reads=[PB(b)], writes=[('ql', qc)])
                        S.op('dve', lambda e, b=b, qc=qc: e.tensor_tensor(out=sq[:, qc, :], in0=pb[b][:, 0:T], in1=ql[:, qc, :], op=ALU.mult),
                             reads=[PB(b), ('ql', qc)], writes=[('sq', qc)])
                    b = mmring.next()
                    for qc in range(nch):
                        S.op('pe', lambda e, b=b, qc=qc: e.matmul(pb[b][:, 0:T], lhsT=ones_t[:], rhs=sq[:, qc, :], start=(qc == 0), stop=(qc == nch - 1)),
                             reads=[onesres, ('sq', qc)], writes=[PB(b)], inc=(qc == nch - 1))
                    S.op('act', lambda e, b=b: e.activation(out=rs[:], in_=pb[b][:, 0:T], func=AF.Sqrt, bias=cols[:, 127:128]),
                         reads=[PB(b), 'cols'], writes=['rs'])
                    S.op('dve', lambda e: e.reciprocal(out=rs[:], in_=rs[:]), reads=['rs'], writes=['rs'])
                    for qc in range(nch):
                        S.op('dve', lambda e, qc=qc: e.scalar_tensor_tensor(out=outt[:, qc, :], in0=ql[:, qc, :], scalar=cols[:, gco + qc:gco + qc + 1], in1=rs[:],
                                                                            op0=ALU.mult, op1=ALU.mult),
                             reads=[('ql', qc), 'cols', 'rs'], writes=[outres])
                    S.dma(dram.rearrange("(o p) t -> p o t", p=128)[:, :, tok0:tok0 + T], outt[:], semk, reads=[outres], writes=[semk + '_d'])
                latent(3, C_Q0, onesq, 'onesq', CO_GQ, qnt, 'qnt', qn_s, 'qns')
                latent(2, C_KV0, oneskv, 'oneskv', CO_GKV, kvnt, 'kvnt', kvn_s, 'kvns')
                if dbg and upto <= 1 and i == 1:
                    dbg_out("qnt", qnt[:], [128, 3, T], BF16, ['qnt'])
                cs_run(tok0, cos_t[:], sin_t[:], 'cos_t', 'sin_t')
                bm = mmring.next()
                proj_mm(bm, C_RM0, 96, hres, hTt)
                S.op('dve', lambda e, bm=bm: e.tensor_tensor(out=t1[64:96, :], in0=pb[bm][64:96, 0:T], in1=cos_t[64:96, :], op=ALU.mult),
                     reads=[PB(bm), 'cos_t'], writes=['t1'])
                br = mmring.next()
                proj_mm(br, C_RR0, 96, hres, hTt)
                S.op('dve', lambda e, br=br: e.tensor_tensor(out=t2[64:96, :], in0=pb[br][64:96, 0:T], in1=sin_t[64:96, :], op=ALU.mult),
                     reads=[PB(br), 'sin_t'], writes=['t2'])
                S.op('dve', lambda e: e.tensor_tensor(out=krt[64:96, :], in0=t1[64:96, :], in1=t2[64:96, :], op=ALU.add),
                     reads=['t1', 't2'], writes=['krt'])
                S.dma(kr_s[:, tok0:tok0 + T], krt[:], 'krs', reads=['krt'], writes=['kr_s'])

            cs_run = make_cs_runner(ph, T, 'cs1')

            ntile1 = 2 if (dbg and upto <= 1) else NT
            for i in range(ntile1):
                phase1_tile(i)
            if dbg and upto <= 1:
                dbg_out("hT", hT[1][:], [128, 8, T], BF16, ['hT1'])
                dbg_out("krt", krt[:], [128, T], BF16, ['krt'])
                dbg_out("mrt", mrt[:], [128, 8, T], BF16, ['mrt'])
                dbg_out("sgt", sgt[:], [128, 8, T], BF16, ['sgt'])
                dbg_out("kvnt", kvnt[:], [128, 2, T], BF16, ['kvnt'])
                dbg_out("cos", cos_t[:], [96, T], F32, ['cos_t'])
                dbg_out("sin", sin_t[:], [96, T], F32, ['sin_t'])
            S.barrier()
        if upto <= 1:
            S.barrier()
            S.emit()
            return nc, dbg_outs

        with ExitStack() as ph:
            T2 = 512
            wqm = sbt(ph, "wqm", [128, 3, 1536], BF16)
            wqr = sbt(ph, "wqr", [128, 3, 1536], BF16)
            wk = sbt(ph, "wk", [128, 2, 1024], BF16)
            wv = sbt(ph, "wv", [128, 2, 1024], BF16)
            qn = sbt(ph, "qn", [128, 3, S_LEN], BF16)
            kvn = sbt(ph, "kvn", [128, 2, S_LEN], BF16)
            kr = sbt(ph, "kr", [128, S_LEN], BF16)
            S.dma(qn[:], qn_s.rearrange("(o p) t -> p o t", p=128), 'p2l', reads=['qns_d'], writes=['qn'])
            S.dma(kvn[:], kvn_s.rearrange("(o p) t -> p o t", p=128), 'p2l', reads=['kvns_d'], writes=['kvn'])
            S.dma(kr[:], kr_s, 'p2l', reads=['kr_s'], writes=['kr'])
            load_weights([(f2(wqm[:]), f2(wuqm_d), 'wqm'), (f2(wqr[:]), f2(wuqr_d), 'wqr'), (f2(wk[:]), f2(wk_d), 'wk'), (f2(wv[:]), f2(wv_d), 'wv')])
            for k in range(3):
                v4 = wqr[:, k, :].rearrange("p (h d) -> p h d", d=96)[:, :, 64:80]
                S.op('dve', lambda e, v4=v4: e.tensor_scalar(out=v4, in0=v4, scalar1=-1.0, scalar2=None, op0=ALU.mult), reads=['wqr'], writes=['wqr'])
            cosf = sbt(ph, "cosf", [96, S_LEN], F32)
            sinf = sbt(ph, "sinf", [96, S_LEN], F32)
            cs2 = make_cs_runner(ph, T2, 'cs2')
            for i in range(S_LEN // T2):
                cs2(i * T2, cosf[:, i * T2:(i + 1) * T2], sinf[:, i * T2:(i + 1) * T2], ('cosf', i), ('sinf', i))
            QT = [sbt(ph, "QT%d" % i, [96, S_LEN], BF16) for i in range(2)]
            KT = [sbt(ph, "KT%d" % i, [96, S_LEN], BF16) for i in range(2)]
            VV = [sbt(ph, "VV%d" % i, [128, 32, 65], BF16) for i in range(2)]
            y4 = [sbt(ph, "y4_%d" % i, [128, 32, 256], BF16) for i in range(2)]
            pT = [sbt(ph, "pT%d" % i, [128, 512], BF16) for i in range(3)]
            rec = [sbt(ph, "rec%d" % i, [128, 4], F32) for i in range(2)]
            tq1 = sbt(ph, "tq1", [96, T2], F32)
            tq2 = sbt(ph, "tq2", [96, T2], F32)
            for par in range(2):
                S.op('pool', lambda e, par=par: e.memset(VV[par][:, :, 64:65], 1.0), writes=[('V', par, i) for i in range(8)])
            sring = Ring([0, 1, 2])
            oring = Ring([3, 4])
            pring = Ring([5, 6, 7])
            ptring = Ring([0, 1, 2])
            SCALE = 1.0 / math.sqrt(96.0)

            def head_proj(h):
                par = h % 2
                for i in range(S_LEN // T2):
                    c0 = i * T2
                    bq = pring.next()
                    for k in range(3):
                        S.op('pe', lambda e, k=k, bq=bq: e.matmul(pb[bq][0:96, :], lhsT=wqm[:, k, h * 96:(h + 1) * 96], rhs=qn[:, k, c0:c0 + T2],
                                                                   start=(k == 0), stop=(k == 2)), reads=['wqm', 'qn'], writes=[PB(bq)], inc=(k == 2))
                    S.op('dve', lambda e, bq=bq: e.tensor_copy(out=QT[par][0:64, c0:c0 + T2], in_=pb[bq][0:64, :]), reads=[PB(bq)], writes=[('QT', par, i)])
                    S.op('dve', lambda e, bq=bq: e.tensor_tensor(out=tq1[64:96, :], in0=pb[bq][64:96, :], in1=cosf[64:96, c0:c0 + T2], op=ALU.mult),
                         reads=[PB(bq), ('cosf', i)], writes=['tq1'])
                    br_ = pring.next()
                    for k in range(3):
                        S.op('pe', lambda e, k=k, br_=br_: e.matmul(pb[br_][0:96, :], lhsT=wqr[:, k, h * 96:(h + 1) * 96], rhs=qn[:, k, c0:c0 + T2],
                                                                     start=(k == 0), stop=(k == 2)), reads=['wqr', 'qn'], writes=[PB(br_)], inc=(k == 2))
                    S.op('dve', lambda e, br_=br_: e.tensor_tensor(out=tq2[64:96, :], in0=pb[br_][64:96, :], in1=sinf[64:96, c0:c0 + T2], op=ALU.mult),
                         reads=[PB(br_), ('sinf', i)], writes=['tq2'])
                    S.op('dve', lambda e: e.tensor_tensor(out=QT[par][64:96, c0:c0 + T2], in0=tq1[64:96, :], in1=tq2[64:96, :], op=ALU.add),
                         reads=['tq1', 'tq2'], writes=[('QT', par, i)])
                    bk = pring.next()
                    for k in range(2):
                        S.op('pe', lambda e, k=k, bk=bk: e.matmul(pb[bk][0:64, :], lhsT=wk[:, k, h * 64:(h + 1) * 64], rhs=kvn[:, k, c0:c0 + T2],
                                                                   start=(k == 0), stop=(k == 1)), reads=['wk', 'kvn'], writes=[PB(bk)], inc=(k == 1))
                    S.op('dve', lambda e, bk=bk: e.tensor_copy(out=KT[par][0:64, c0:c0 + T2], in_=pb[bk][0:64, :]), reads=[PB(bk)], writes=[('KT', par, i)])
                    S.op('pool', lambda e: e.tensor_copy(out=KT[par][64:96, c0:c0 + T2], in_=kr[64:96, c0:c0 + T2]), reads=['kr'], writes=[('KT', par, i)])
                    bv = pring.next()
                    for blk in range(4):
                        for k in range(2):
                            S.op('pe', lambda e, k=k, blk=blk, bv=bv: e.matmul(pb[bv][:, blk * 64:(blk + 1) * 64], lhsT=kvn[:, k, c0 + blk * 128:c0 + (blk + 1) * 128],
                                                                                rhs=wv[:, k, h * 64:(h + 1) * 64], start=(k == 0), stop=(k == 1)),
                                 reads=['wv', 'kvn'], writes=[PB(bv)], inc=(k == 1 and blk == 3))
                    S.op('dve', lambda e, bv=bv: e.tensor_copy(out=VV[par][:, i * 4:(i + 1) * 4, 0:64], in_=pb[bv][:, 0:256].rearrange("p (b d) -> p b d", d=64)),
                         reads=[PB(bv)], writes=[('V', par, i)])

            def head_attn(h):
                par = h % 2
                g = h // 4
                hl = h % 4
                yb4 = y4[g % 2]
                y4res = 'y4_%d' % (g % 2)
                for i in range(S_LEN // T2):
                    ob = oring.next()
                    nj = 4 * i + 4
                    state = {'first': True}

                    def emit_pv(j, pt, qlo):
                        qbs = list(range(qlo // 128, 4))
                        for qb in qbs:
                            st_ = state['first']
                            state['first'] = False
                            S.op('pe', lambda e, qb=qb, st_=st_: e.matmul(pb[ob][:, qb * 128:qb * 128 + 65], lhsT=pT[pt][:, qb * 128:(qb + 1) * 128], rhs=VV[par][:, j, :],
                                                                            start=st_, stop=(j == 4 * i + qb), skip_group_check=True),
                                 reads=[('pT', pt), ('V', par, j // 4)], writes=[PB(ob)], inc=(qb == qbs[-1]))
                    prev = None
                    for j in range(nj):
                        qlo = max(0, j - 4 * i) * 128
                        sb_ = sring.next()
                        S.op('pe', lambda e, j=j, qlo=qlo, sb_=sb_: e.matmul(pb[sb_][:, qlo:512], lhsT=KT[par][:, j * 128:(j + 1) * 128],
                                                                               rhs=QT[par][:, i * 512 + qlo:(i + 1) * 512], start=True, stop=True),
                             reads=[('KT', par, j // 4), ('QT', par, i)], writes=[PB(sb_)])
                        pt = ptring.next()
                        S.op('act', lambda e, qlo=qlo, sb_=sb_, pt=pt: e.activation(out=pT[pt][:, qlo:512], in_=pb[sb_][:, qlo:512], func=AF.Exp, scale=SCALE),
                             reads=[PB(sb_)], writes=[('pT', pt)])
                        if j >= 4 * i:
                            S.op('pool', lambda e, qlo=qlo, pt=pt: e.affine_select(out=pT[pt][:, qlo:qlo + 128], in_=pT[pt][:, qlo:qlo + 128], pattern=[[1, 128]],
                                                                                    compare_op=ALU.is_ge, fill=0.0, base=0, channel_multiplier=-1),
                                 reads=[('pT', pt)], writes=[('pT', pt)])
                        if prev is not None:
                            emit_pv(*prev)
                        prev = (j, pt, qlo)
                    emit_pv(*prev)
                    rc = rec[i % 2]
                    rcres = 'rec%d' % (i % 2)
                    S.op('dve', lambda e, rc=rc: e.reciprocal(out=rc[:, 0:4], in_=pb[ob][:, :].rearrange("p (q d) -> p q d", d=128)[:, :, 64]),
                         reads=[PB(ob)], writes=[rcres])
                    for qb in range(4):
                        S.op('dve', lambda e, qb=qb, rc=rc: e.tensor_scalar(out=yb4[:, 4 * i + qb, hl * 64:(hl + 1) * 64], in0=pb[ob][:, qb * 128:qb * 128 + 64],
                                                                              scalar1=rc[:, qb:qb + 1], scalar2=None, op0=ALU.mult),
                             reads=[PB(ob), rcres], writes=[y4res])
                if hl == 3:
                    S.dma(ymla_s.rearrange("(b p) f -> p b f", p=128)[:, :, g * 256:(g + 1) * 256], yb4[:], 'y4s%d' % (g % 2), reads=[y4res], writes=['ymla_s'])

            nheads = NH
            head_proj(0)
            for h in range(nheads):
                if h + 1 < nheads:
                    head_proj(h + 1)
                head_attn(h)
            S.barrier()
        if dbg:
            t = nc.dram_tensor("dbg2_y_mla", [S_LEN, D], BF16, kind="ExternalOutput").ap()
            S.dma(t, ymla_s, 'dbg2y', reads=['ymla_s'], writes=['dbg2y'])
        if upto <= 2:
            S.barrier()
            S.emit()
            return nc, dbg_outs

        with ExitStack() as ph:
            T3 = 512
            wpm = sbt(ph, "wpm", [128, 8, 1024], BF16)
            wo = sbt(ph, "wo", [128, 8, 1024], BF16)
            load_weights([(f2(wpm[:]), f2(wpm_d), 'wpm'), (f2(wo[:]), f2(wo_d), 'wo')])
            g1bc = load_bc(ph, "g1bc", 2048)
            ytok = [sbt(ph, "ytok%d" % i, [128, 4, 1024], BF16) for i in range(2)]
            sgt3 = [sbt(ph, "sgt3_%d" % i, [128, 8, T3], BF16) for i in range(2)]
            mrt3 = [sbt(ph, "mrt3_%d" % i, [128, 8, T3], BF16) for i in range(2)]
            yT = sbt(ph, "yT", [128, 8, T3], BF16)
            mg = sbt(ph, "mg", [128, 8, T3], BF16)
            tmg = [sbt(ph, "tmg%d" % i, [128, T3], F32) for i in range(2)]
            xin3 = [sbt(ph, "xin3_%d" % i, [128, 1024], F32) for i in range(2)]
            tmo = [sbt(ph, "tmo%d" % i, [128, 1024], F32) for i in range(2)]
            mm3 = Ring([2, 3, 4, 5, 6, 7])
            xr3 = Ring([0, 1])
            for i in range(S_LEN // T3):
                c0 = i * T3
                sl = i % 2
                S.dma(ytok[sl][:], ymla_s.rearrange("(b p) f -> p b f", p=128)[:, 4 * i:4 * i + 4, :], 'yt%d' % sl, reads=['ymla_s'], writes=['ytok%d' % sl])
                S.dma(sgt3[sl][:], sgm_s.rearrange("(o p) t -> p o t", p=128)[:, :, c0:c0 + T3], 'sg3%d' % sl, reads=['sgm_s'], writes=['sgt3_%d' % sl])
                S.dma(mrt3[sl][:], mrnn_s.rearrange("(o p) t -> p o t", p=128)[:, :, c0:c0 + T3], 'mr3%d' % sl, reads=['mrnn_s'], writes=['mrt3_%d' % sl])
                for blk in range(4):
                    tb = blk % 2
                    pbv = pb[tb][:].bitcast(BF16)
                    for k in range(8):
                        S.op('pe', lambda e, k=k, blk=blk, pbv=pbv: e.transpose(out=pbv[:, k * 128:(k + 1) * 128], in_=ytok[sl][:, blk, k * 128:(k + 1) * 128], identity=identb[:]),
                             reads=['ytok%d' % sl, 'identb'], writes=[PB(tb)], inc=(k == 7))
                    S.op('act', lambda e, blk=blk, pbv=pbv: e.copy(out=yT[:, :, blk * 128:(blk + 1) * 128], in_=pbv.rearrange("p (k t) -> p k t", k=8)),
                         reads=[PB(tb)], writes=['yT'])
                for oc in range(8):
                    b = mm3.next()
                    for k in range(8):
                        S.op('pe', lambda e, k=k, oc=oc, b=b: e.matmul(pb[b][:, :], lhsT=wpm[:, k, oc * 128:(oc + 1) * 128], rhs=yT[:, k, :], start=(k == 0), stop=(k == 7)),
                             reads=['wpm', 'yT'], writes=[PB(b)], inc=(k == 7))
                    tm = tmg[oc % 2]
                    S.op('dve', lambda e, oc=oc, b=b, tm=tm: e.tensor_tensor(out=tm[:], in0=pb[b][:, :], in1=sgt3[sl][:, oc, :], op=ALU.mult),
                         reads=[PB(b), 'sgt3_%d' % sl], writes=['tmg%d' % (oc % 2)])
                    S.op('pool', lambda e, oc=oc, tm=tm: e.tensor_tensor(out=mg[:, oc, :], in0=tm[:], in1=mrt3[sl][:, oc, :], op=ALU.add),
                         reads=['tmg%d' % (oc % 2), 'mrt3_%d' % sl], writes=['mg'])
                for blk in range(4):
                    xi = xr3.next()
                    r0 = c0 + blk * 128
                    S.dma(xin3[xi][:], x_d[r0:r0 + 128, :], 'xin3_%d' % xi, writes=['xin3_%d' % xi])
                    to = tmo[xi]
                    for half in range(2):
                        b = mm3.next()
                        for k in range(8):
                            S.op('pe', lambda e, k=k, blk=blk, half=half, b=b: e.matmul(pb[b][:, :], lhsT=mg[:, k, blk * 128:(blk + 1) * 128], rhs=wo[:, k, half * 512:(half + 1) * 512],
                                                                                       start=(k == 0), stop=(k == 7)),
                                 reads=['wo', 'mg'], writes=[PB(b)], inc=(k == 7))
                        S.op('dve', lambda e, half=half, b=b, to=to: e.tensor_tensor(out=to[:, half * 512:(half + 1) * 512], in0=pb[b][:, :], in1=g1bc[:, half * 512:(half + 1) * 512], op=ALU.mult),
                             reads=[PB(b), 'g1bc'], writes=['tmo%d' % xi])
                    S.op('pool', lambda e, xi=xi, to=to: e.tensor_tensor(out=to[:], in0=to[:], in1=xin3[xi][:], op=ALU.add),
                         reads=['tmo%d' % xi, 'xin3_%d' % xi], writes=['tmo%d' % xi])
                    S.dma(x1_s[r0:r0 + 128, :], to[:], 'x1s%d' % xi, reads=['tmo%d' % xi], writes=['x1_s'])
            S.barrier()
        if dbg:
            t = nc.dram_tensor("dbg2_x1", [S_LEN, D], F32, kind="ExternalOutput").ap()
            S.dma(t, x1_s, 'dbg2x', reads=['x1_s'], writes=['dbg2x'])
        if upto <= 3:
            S.barrier()
            S.emit()
            return nc, dbg_outs

        with ExitStack() as ph:
            T4 = 256
            NB4 = T4 // 128
            wup = sbt(ph, "wup", [128, 8, 2 * D_FF], BF16)
            wdn = sbt(ph, "wdn", [128, 22, 1024], BF16)
            load_weights([(f2(wup[:]), f2(wup_d), 'wup'), (f2(wdn[:]), f2(wdn_d), 'wdn')])
            tmpo = sbt(ph, "tmpo", [128, 1024], F32)
            gm2 = make_gm(ph, "gm2", 4096, 6144 + 1024, tmpo, 'tmpo')
            sh2 = load_bc(ph, "sh2", 3072)
            g2bc = load_bc(ph, "g2bc", 5120)
            fgbc = load_row_bc(ph, "fgbc", 6144 + 2048)
            fcw = sbt(ph, "fcw", [128, 132], F32)
            fcb = sbt(ph, "fcb", [128, 44], F32)
            S.dma(fcw[:], fcw_d, 'p4c', writes=['fcw'])
            S.dma(fcb[:], fcb_d, 'p4c', writes=['fcb'])
            carry2 = sbt(ph, "carry2", [128, 44, 2], F32)
            S.op('pool', lambda e: e.memset(carry2[:], 0.0), writes=['carry2'])
            x1t = [sbt(ph, "x1t%d" % i, [128, 1024], F32) for i in range(2)]
            junk4 = sbt(ph, "junk4", [128, 1024], BF16)
            hb4 = [sbt(ph, "hb4_%d" % i, [128, 1024], BF16) for i in range(2)]
            ss4 = [sbt(ph, "ss4_%d" % i, [128, 4], F32) for i in range(2)]
            ss5 = [sbt(ph, "ss5_%d" % i, [128, 4], F32) for i in range(2)]
            h2T = [sbt(ph, "h2T%d" % i, [128, 8, T4], BF16) for i in range(2)]
            xs2 = [sbt(ph, "xs2_%d" % i, [128, T4 + 2], F32) for i in range(2)]
            ug = [sbt(ph, "ug%d" % i, [128, T4], F32) for i in range(2)]
            uv = [sbt(ph, "uv%d" % i, [128, T4], F32) for i in range(2)]
            prod = sbt(ph, "prod", [128, 22, T4], BF16)
            mm4 = Ring([2, 3, 4, 5, 6, 7])
            xsr4 = Ring([0, 1])
            for i in range(S_LEN // T4):
                c0 = i * T4
                hTt = h2T[i % 2]
                hres = 'h2T%d' % (i % 2)
                for blk in range(NB4):
                    r0 = c0 + blk * 128
                    S.dma(x1t[blk][:], x1_s[r0:r0 + 128, :], 'x1t%d' % blk, reads=['x1_s'], writes=['x1t%d' % blk])
                    norm_to_featmajor(x1t[blk][:], 'x1t%d' % blk, gm2, 'gm2', sh2, 'sh2', tmpo, 'tmpo', junk4, 'junk4', hb4[blk], 'hb4_%d' % blk,
                                      ss4[blk], 'ss4_%d' % blk, blk, hTt[:, :, blk * 128:(blk + 1) * 128], hres)
                for c in range(22):
                    ubuf = {}
                    for part, ch in (('g', c), ('v', c + 22)):
                        b = mm4.next()
                        for k in range(8):
                            S.op('pe', lambda e, k=k, ch=ch, b=b: e.matmul(pb[b][:, 0:T4], lhsT=wup[:, k, ch * 128:(ch + 1) * 128], rhs=hTt[:, k, :], start=(k == 0), stop=(k == 7)),
                                 reads=['wup', hres], writes=[PB(b)], inc=(k == 7))
                        xsi = xsr4.next()
                        xb = xs2[xsi]
                        xres = 'xs2_%d' % xsi
                        S.op('pool', lambda e, ch=ch, xb=xb: e.tensor_copy(out=xb[:, 0:2], in_=carry2[:, ch, :]), reads=['carry2'], writes=[xres])
                        S.op('act', lambda e, b=b, xb=xb: e.copy(out=xb[:, 2:T4 + 2], in_=pb[b][:, 0:T4]), reads=[PB(b)], writes=[xres])
                        S.op('pool', lambda e, ch=ch, xb=xb: e.tensor_copy(out=carry2[:, ch, :], in_=xb[:, T4:T4 + 2]), reads=[xres], writes=['carry2'])
                        u = (ug if part == 'g' else uv)[c % 2]
                        ures = ('ug%d' if part == 'g' else 'uv%d') % (c % 2)
                        S.op('dve', lambda e, ch=ch, xb=xb, u=u: e.tensor_scalar(out=u[:], in0=xb[:, 0:T4], scalar1=fcw[:, ch * 3:ch * 3 + 1], scalar2=fcb[:, ch:ch + 1],
                                                                                  op0=ALU.mult, op1=ALU.add), reads=[xres, 'fcw', 'fcb'], writes=[ures])
                        for kk in (1, 2):
                            S.op('dve', lambda e, ch=ch, xb=xb, u=u, kk=kk: e.scalar_tensor_tensor(out=u[:], in0=xb[:, kk:T4 + kk], scalar=fcw[:, ch * 3 + kk:ch * 3 + kk + 1], in1=u[:],
                                                                                                  op0=ALU.mult, op1=ALU.add), reads=[xres, 'fcw', ures], writes=[ures])
                        ubuf[part] = (u, ures)
                    (ugb, ugres), (uvb, uvres) = ubuf['g'], ubuf['v']
                    S.op('act', lambda e, ugb=ugb: e.activation(out=ugb[:], in_=ugb[:], func=AF.Silu), reads=[ugres], writes=[ugres])
                    S.op('pool', lambda e, c=c, ugb=ugb, uvb=uvb: e.tensor_tensor(out=prod[:, c, :], in0=ugb[:], in1=uvb[:], op=ALU.mult),
                         reads=[ugres, uvres], writes=[('prod', c)])
                for blk in range(NB4):
                    r0 = c0 + blk * 128
                    for half in range(2):
                        b = mm4.next()
                        for c in range(22):
                            S.op('pe', lambda e, c=c, blk=blk, half=half, b=b: e.matmul(pb[b][:, :], lhsT=prod[:, c, blk * 128:(blk + 1) * 128], rhs=wdn[:, c, half * 512:(half + 1) * 512],
                                                                                       start=(c == 0), stop=(c == 21)),
                                 reads=['wdn', ('prod', c)], writes=[PB(b)], inc=(c == 21))
                        S.op('dve', lambda e, half=half, b=b: e.tensor_tensor(out=tmpo[:, half * 512:(half + 1) * 512], in0=pb[b][:, :], in1=g2bc[:, half * 512:(half + 1) * 512], op=ALU.mult),
                             reads=[PB(b), 'g2bc'], writes=['tmpo'])
                    xb_ = x1t[blk]
                    xbres = 'x1t%d' % blk
                    S.op('pool', lambda e, xb_=xb_: e.tensor_tensor(out=xb_[:], in0=tmpo[:], in1=xb_[:], op=ALU.add), reads=['tmpo', xbres], writes=[xbres])
                    s5 = ss5[blk]
                    s5r = 'ss5_%d' % blk
                    S.op('act', lambda e, xb_=xb_, s5=s5: e.activation(out=junk4[:], in_=xb_[:], func=AF.Square, accum_out=s5[:, 0:1]), reads=[xbres], writes=['junk4', s5r])
                    S.op('act', lambda e, s5=s5: e.activation(out=s5[:, 1:2], in_=s5[:, 0:1], func=AF.Sqrt, scale=1.0 / D, bias=cols[:, 127:128]), reads=[s5r, 'cols'], writes=[s5r])
                    S.op('dve', lambda e, s5=s5: e.reciprocal(out=s5[:, 2:3], in_=s5[:, 1:2]), reads=[s5r], writes=[s5r])
                    S.op('dve', lambda e, xb_=xb_, s5=s5: e.scalar_tensor_tensor(out=tmpo[:], in0=xb_[:], scalar=s5[:, 2:3], in1=fgbc[:], op0=ALU.mult, op1=ALU.mult),
                         reads=[xbres, s5r, 'fgbc'], writes=['tmpo'])
                    S.dma(out_d[r0:r0 + 128, :], tmpo[:], 'outd', reads=['tmpo'], writes=['out_d'])
            S.barrier()
        S.barrier()
        S.emit()
        return nc, dbg_outs


def _pk(w, kchunks):
    K, N = w.shape
    return np.ascontiguousarray(w.reshape(kchunks, 128, N).transpose(1, 0, 2))


def _col(v, nch):
    return np.ascontiguousarray(v.reshape(nch, 128).T)


def prep_shared(inp):
    f = lambda a: np.asarray(a, dtype=np.float32)
    sh = {}
    sh["w_ada"] = _pk(f(inp["w_ada"])[0], 8)
    rows = np.concatenate([f(inp["b_ada"])[0], f(inp["norm1_g"])[0], f(inp["norm2_g"])[0], f(inp["final_g"])]).reshape(1, -1)
    sh["rows"] = np.ascontiguousarray(rows)
    w_in = f(inp["w_in"])[0]
    wp = np.zeros((1024, NP_IN), np.float32)
    wp[:, 0:1920] = w_in[:, 0:1920]
    rope = w_in[:, 1920:1952]
    wp[:, C_RM0 + 64:C_RM0 + 96] = rope
    wp[:, C_RR0 + 64:C_RR0 + 80] = rope[:, 16:32]
    wp[:, C_RR0 + 80:C_RR0 + 96] = rope[:, 0:16]
    wp[:, C_GR0:C_GR0 + 1024] = w_in[:, 1952:2976]
    wp[:, C_GM0:C_GM0 + 1024] = w_in[:, 2976:4000]
    sh["w_in_p"] = _pk(wp, 8)

    def bd(w):
        w = w[0]
        o = np.zeros((128, 10, 128), np.float32)
        for c in range(10):
            o[0:64, c, 0:64] = w[2 * c]
            o[64:128, c, 64:128] = w[2 * c + 1]
        return o
    sh["w_ga"] = bd(f(inp["w_gate_a"]))
    sh["w_gx"] = bd(f(inp["w_gate_x"]))
    wuq = f(inp["w_uq"])[0]
    sh["w_uq_m"] = _pk(wuq, 3)
    wr = np.zeros_like(wuq)
    for h in range(NH):
        wr[:, h * 96 + 64:h * 96 + 80] = wuq[:, h * 96 + 80:h * 96 + 96]
        wr[:, h * 96 + 80:h * 96 + 96] = wuq[:, h * 96 + 64:h * 96 + 80]
    sh["w_uq_r"] = _pk(wr, 3)
    wukv = f(inp["w_ukv"])[0].reshape(256, NH, 128)
    sh["w_k"] = _pk(np.ascontiguousarray(wukv[:, :, 0:64]).reshape(256, 1024), 2)
    sh["w_v"] = _pk(np.ascontiguousarray(wukv[:, :, 64:128]).reshape(256, 1024), 2)
    sh["w_pr"] = _pk(f(inp["w_proj_rnn"])[0], 10)
    sh["w_pm"] = _pk(f(inp["w_proj_mla"])[0], 8)
    sh["w_o"] = _pk(f(inp["w_out"])[0], 8)
    sh["w_up"] = _pk(f(inp["w_up"])[0], 8)
    sh["w_dn"] = _pk(f(inp["w_down"])[0], 22)
    fcw = f(inp["ffn_conv_w"])[0]
    sh["fcw"] = np.ascontiguousarray(fcw.reshape(3, 44, 128).transpose(2, 1, 0)).reshape(128, 132)
    sh["fcb"] = _col(f(inp["ffn_conv_b"])[0], 44)
    sh["ident"] = np.eye(128, dtype=np.float32)
    cols = np.zeros((128, 128), np.float32)
    cw = f(inp["conv_w"])[0]
    cols[:, 0:40] = cw.reshape(4, 10, 128).transpose(2, 1, 0).reshape(128, 40)
    cols[:, 40:50] = _col(f(inp["conv_b"])[0], 10)
    cols[:, 50:60] = _col(f(inp["b_gate_a"])[0], 10)
    cols[:, 60:70] = _col(f(inp["b_gate_x"])[0], 10)
    cols[:, 70:80] = _col(f(inp["lru_param"])[0], 10)
    cols[:, 80:83] = _col(f(inp["q_norm_g"])[0], 3)
    cols[:, 83:85] = _col(f(inp["kv_norm_g"])[0], 2)
    half = 16
    inv_freq = (np.float32(10000.0) ** (-np.arange(half, dtype=np.float32) / np.float32(half))).astype(np.float32)
    cols[64:80, 85] = inv_freq
    cols[80:96, 85] = inv_freq
    cols[:, 126] = 1.0
    cols[:, 127] = EPS
    sh["cols"] = cols
    return sh


def prep_core(inp, b):
    d = {}
    d["x"] = np.ascontiguousarray(np.asarray(inp["x"], dtype=np.float32)[b])
    d["c_col"] = _col(np.asarray(inp["c"], dtype=np.float32)[b], 8)
    d["pos"] = np.ascontiguousarray(np.asarray(inp["positions"], dtype=np.int32)[b].reshape(1, S_LEN))
    return d


def kernel(**inputs):
    nc, _ = build_program()
    sh = prep_shared(inputs)
    in_maps = []
    for b in range(8):
        m = dict(sh)
        m.update(prep_core(inputs, b))
        in_maps.append(m)
    res = run_bass_kernel_spmd(nc, in_maps, core_ids=list(range(8)))
    return np.stack([np.asarray(r["out"]).reshape(S_LEN, D) for r in res.results], axis=0).astype(np.float32)
```

```python
import math
from contextlib import ExitStack

import numpy as np
import concourse.bass as bass
import concourse.mybir as mybir
from concourse.bass_utils import run_bass_kernel_spmd

F32 = mybir.dt.float32
BF16 = mybir.dt.bfloat16
I32 = mybir.dt.int32
AF = mybir.ActivationFunctionType
ALU = mybir.AluOpType

S_LEN = 4096
D = 1024
D_RNN = 1280
NH = 16
D_FF = 2816
EPS = 1e-6
NP_IN = 4160
C_RNN0, C_Q0, C_KV0, C_RM0, C_RR0, C_GR0, C_GM0 = 0, 1280, 1664, 1920, 2016, 2112, 3136
TWO_PI = 2.0 * math.pi
CW1 = 6.28125
CW2 = TWO_PI - CW1
PI_SAFE = 3.141592


class Sched:
    ENG = ('pe', 'act', 'dve', 'pool', 'sp')

    def __init__(self, nc, es):
        self.nc = nc
        self.es = es
        self.streams = {e: [] for e in self.ENG}
        self.esem = {e: es.enter_context(nc.semaphore("prog_" + e)) for e in ('pe', 'act', 'dve', 'pool')}
        self.ecnt = {e: 0 for e in self.esem}
        self.known = {e: {} for e in self.ENG}
        self.lastw = {}
        self.rd = {}
        self.dsem = {}
        self.dcnt = {}
        self.nops = 0

    def _dma_sem(self, key):
        if key not in self.dsem:
            self.dsem[key] = self.es.enter_context(self.nc.semaphore("dma_" + str(key)))
            self.dcnt[key] = 0
        return self.dsem[key]

    def _resolve(self, eng, toks):
        need = {}
        for (kind, key, val) in toks:
            if kind == 'e':
                if key == eng and eng == 'pe':
                    continue
                k = ('e', key)
                v = val
            else:
                k = ('d', key)
                v = self.dcnt[key]
            if need.get(k, 0) < v:
                need[k] = v
        out = []
        for k, v in need.items():
            if self.known[eng].get(k, 0) >= v:
                continue
            self.known[eng][k] = v
            sem = self.esem[k[1]] if k[0] == 'e' else self.dsem[k[1]]
            out.append((sem, v))
        return out

    def _deps(self, eng, reads, writes):
        toks = []
        for r in reads:
            if r in self.lastw:
                toks.append(self.lastw[r])
        for w in writes:
            if w in self.lastw:
                toks.append(self.lastw[w])
            toks.extend(self.rd.get(w, ()))
        return self._resolve(eng, toks)

    def _commit(self, tok, reads, writes):
        for r in reads:
            self.rd.setdefault(r, []).append(tok)
        for w in writes:
            self.lastw[w] = tok
            self.rd[w] = []

    def op(self, eng, fn, reads=(), writes=(), inc=True):
        rec = _Rec()
        fn(rec)
        assert len(rec.calls) == 1
        mname, margs, mkw = rec.calls[0]

        def fn(e, mname=mname, margs=margs, mkw=mkw):
            return getattr(e, mname)(*margs, **mkw)
        waits = self._deps(eng, reads, writes)
        self.nops += 1
        sem = self.esem[eng]
        if inc:
            self.ecnt[eng] += 1

        def run(e, waits=waits, fn=fn, sem=sem, inc=inc):
            for (s, v) in waits:
                e.wait_ge(s, v)
            ins = fn(e)
            if inc:
                ins.then_inc(sem, 1)
        self.streams[eng].append(run)
        self._commit(('e', eng, self.ecnt[eng] if inc else self.ecnt[eng] + 1), reads, writes)

    def dma(self, out, in_, semkey, reads=(), writes=(), q='sp'):
        waits = self._deps(q, reads, writes)
        sem = self._dma_sem(semkey)
        self.dcnt[semkey] += 16
        self.nops += 1

        def run(e, waits=waits, sem=sem, out=out, in_=in_):
            for (s, v) in waits:
                e.wait_ge(s, v)
            e.dma_start(out=out, in_=in_).then_inc(sem, 16)
        self.streams[q].append(run)
        self._commit(('d', semkey, None), reads, writes)

    def barrier(self):
        toks = [('e', e, self.ecnt[e]) for e in self.esem if self.ecnt[e] > 0]
        toks += [('d', k, None) for k in self.dsem if self.dcnt[k] > 0]
        for eng in self.ENG:
            waits = self._resolve(eng, [t for t in toks if not (t[0] == 'e' and t[1] == eng)])

            def run(e, waits=waits):
                for (s, v) in waits:
                    e.wait_ge(s, v)
            self.streams[eng].append(run)

    def emit(self):
        nc = self.nc
        st = self.streams
        with nc.Block() as block:
            @block.tensor
            def _(e):
                for f in st['pe']:
                    f(e)

            @block.scalar
            def _(e):
                for f in st['act']:
                    f(e)

            @block.vector
            def _(e):
                for f in st['dve']:
                    f(e)

            @block.gpsimd
            def _(e):
                for f in st['pool']:
                    f(e)

            @block.sync
            def _(e):
                for f in st['sp']:
                    f(e)


class _Rec:
    def __init__(self):
        self.calls = []

    def __getattr__(self, name):
        def f(*a, **kw):
            self.calls.append((name, a, kw))
        return f


class Ring:
    def __init__(self, items):
        self.items = list(items)
        self.i = 0

    def next(self):
        it = self.items[self.i % len(self.items)]
        self.i += 1
        return it


def bcast_rows(ap2d, nparts):
    n = ap2d.shape[-1]
    return bass.AP(ap2d.tensor, ap2d.offset, [[0, nparts], [1, n]])


def build_program(upto=99, dbg=False):
    nc = bass.Bass("TRN2", target_bir_lowering=False)
    dbg_outs = {}

    def din(name, shape, dt=F32):
        return nc.dram_tensor(name, shape, dt, kind="ExternalInput").ap()

    x_d = din("x", [S_LEN, D])
    c_d = din("c_col", [128, 8])
    pos_d = din("pos", [1, S_LEN], I32)
    wada_d = din("w_ada", [128, 8, 6144])
    rows_d = din("rows", [1, 6144 + 3 * 1024])
    cols_d = din("cols", [128, 128])
    win_d = din("w_in_p", [128, 8, NP_IN])
    wga_d = din("w_ga", [128, 10, 128])
    wgx_d = din("w_gx", [128, 10, 128])
    wuqm_d = din("w_uq_m", [128, 3, 1536])
    wuqr_d = din("w_uq_r", [128, 3, 1536])
    wk_d = din("w_k", [128, 2, 1024])
    wv_d = din("w_v", [128, 2, 1024])
    wpr_d = din("w_pr", [128, 10, 1024])
    wpm_d = din("w_pm", [128, 8, 1024])
    wo_d = din("w_o", [128, 8, 1024])
    wup_d = din("w_up", [128, 8, 2 * D_FF])
    wdn_d = din("w_dn", [128, 22, 1024])
    fcw_d = din("fcw", [128, 44 * 3])
    fcb_d = din("fcb", [128, 44])
    ident_d = din("ident", [128, 128])
    out_d = nc.dram_tensor("out", [S_LEN, D], F32, kind="ExternalOutput").ap()

    mod_s = nc.dram_tensor("mod_s", [1, 6144], F32).ap()
    mrnn_s = nc.dram_tensor("mrnn_s", [D, S_LEN], BF16).ap()
    sgm_s = nc.dram_tensor("sgm_s", [D, S_LEN], BF16).ap()
    qn_s = nc.dram_tensor("qn_s", [384, S_LEN], BF16).ap()
    kvn_s = nc.dram_tensor("kvn_s", [256, S_LEN], BF16).ap()
    kr_s = nc.dram_tensor("kr_s", [128, S_LEN], BF16).ap()
    ymla_s = nc.dram_tensor("ymla_s", [S_LEN, D], BF16).ap()
    x1_s = nc.dram_tensor("x1_s", [S_LEN, D], F32).ap()
    yrnn_s = nc.dram_tensor("yrnn_s", [D_RNN, S_LEN], BF16).ap()
    sgr_s = nc.dram_tensor("sgr_s", [D, S_LEN], BF16).ap()
    cos_s = nc.dram_tensor("cos_s", [96, S_LEN], F32).ap()
    sin_s = nc.dram_tensor("sin_s", [96, S_LEN], F32).ap()

    CO_CONVW, CO_CONVB, CO_BA, CO_BX, CO_LRU, CO_GQ, CO_GKV, CO_INVF = 0, 40, 50, 60, 70, 80, 83, 85

    with ExitStack() as es:
        S = Sched(nc, es)

        def sbt(st, name, shape, dt):
            return st.enter_context(nc.sbuf_tensor("s_" + name, shape, dt))

        pb = [es.enter_context(nc.psum_tensor("pb%d" % i, [128, 512], F32)) for i in range(8)]
        PB = lambda i: ('pb', i)

        def dbg_out(name, ap_sb, shape, dt, res):
            if not dbg:
                return
            t = nc.dram_tensor("dbg_" + name, list(shape), dt, kind="ExternalOutput").ap()
            S.dma(t, ap_sb, 'dbg_' + name, reads=res, writes=['dbgd_' + name])
            dbg_outs[name] = t

        ident = sbt(es, "ident", [128, 128], F32)
        identb = sbt(es, "identb", [128, 128], BF16)
        cols = sbt(es, "cols", [128, 128], F32)
        S.dma(ident[:], ident_d, 'c0', writes=['ident'])
        S.dma(cols[:], cols_d, 'c0', writes=['cols'])
        S.op('dve', lambda e: e.tensor_copy(out=identb[:], in_=ident[:]), reads=['ident'], writes=['identb'])

        stg_ctr = [0]

        def load_weights(items):
            PIECE = 6144
            with ExitStack() as st:
                stg_ctr[0] += 1
                stg = [sbt(st, "stg%d_%d" % (stg_ctr[0], i), [128, PIECE], F32) for i in range(2)]
                engs = Ring(['act', 'pool', 'dve', 'act'])
                cnt = 0
                for d2, s2, res in items:
                    Fd = d2.shape[1]
                    for o in range(0, Fd, PIECE):
                        n = min(PIECE, Fd - o)
                        si = cnt % 2
                        cnt += 1
                        S.dma(stg[si][:, 0:n], s2[:, o:o + n], 'stg%d' % si, writes=['stg%d' % si])
                        eng = engs.next()
                        if eng == 'act':
                            S.op(eng, lambda e: e.copy(out=d2[:, o:o + n], in_=stg[si][:, 0:n]), reads=['stg%d' % si], writes=[res])
                        else:
                            S.op(eng, lambda e: e.tensor_copy(out=d2[:, o:o + n], in_=stg[si][:, 0:n]), reads=['stg%d' % si], writes=[res])
                S.barrier()

        def f2(ap):
            return ap.rearrange("p k n -> p (k n)")

        def make_cs_runner(st, n, tag):
            posi = sbt(st, tag + "_posi", [96, n], I32)
            ang = sbt(st, tag + "_ang", [96, n], F32)
            kf = sbt(st, tag + "_kf", [96, n], F32)
            ki = sbt(st, tag + "_ki", [96, n], I32)
            R = lambda s_: tag + s_

            def wrap():
                S.op('dve', lambda e: e.tensor_scalar(out=kf[:], in0=ang[:], scalar1=math.pi, scalar2=-TWO_PI, op0=ALU.is_gt, op1=ALU.mult),
                     reads=[R('ang')], writes=[R('kf')])
                S.op('dve', lambda e: e.tensor_tensor(out=ang[:], in0=ang[:], in1=kf[:], op=ALU.add), reads=[R('ang'), R('kf')], writes=[R('ang')])
                S.op('dve', lambda e: e.tensor_scalar(out=ang[:], in0=ang[:], scalar1=PI_SAFE, scalar2=-PI_SAFE, op0=ALU.min, op1=ALU.max),
                     reads=[R('ang')], writes=[R('ang')])

            def run(tok0, cos_ap, sin_ap, cosres, sinres):
                S.dma(posi[:], bcast_rows(pos_d[0:1, tok0:tok0 + n], 96), 'pos_' + tag, writes=[R('posi')])
                S.op('dve', lambda e: e.tensor_copy(out=ang[:], in_=posi[:]), reads=[R('posi')], writes=[R('ang')])
                S.op('dve', lambda e: e.tensor_scalar(out=ang[:], in0=ang[:], scalar1=cols[0:96, CO_INVF:CO_INVF + 1], scalar2=None,
                                                      op0=ALU.mult), reads=[R('ang'), 'cols'], writes=[R('ang')])
                S.op('dve', lambda e: e.tensor_scalar(out=kf[:], in0=ang[:], scalar1=1.0 / TWO_PI, scalar2=None, op0=ALU.mult),
                     reads=[R('ang')], writes=[R('kf')])
                S.op('dve', lambda e: e.tensor_copy(out=ki[:], in_=kf[:]), reads=[R('kf')], writes=[R('ki')])
                S.op('dve', lambda e: e.tensor_copy(out=kf[:], in_=ki[:]), reads=[R('ki')], writes=[R('kf')])
                S.op('dve', lambda e: e.scalar_tensor_tensor(out=ang[:], in0=kf[:], scalar=-CW1, in1=ang[:], op0=ALU.mult, op1=ALU.add),
                     reads=[R('kf'), R('ang')], writes=[R('ang')])
                S.op('dve', lambda e: e.scalar_tensor_tensor(out=ang[:], in0=kf[:], scalar=-CW2, in1=ang[:], op0=ALU.mult, op1=ALU.add),
                     reads=[R('kf'), R('ang')], writes=[R('ang')])
                wrap()
                S.op('act', lambda e: e.activation(out=sin_ap, in_=ang[:], func=AF.Sin), reads=[R('ang')], writes=[sinres])
                S.op('dve', lambda e: e.tensor_scalar(out=ang[:], in0=ang[:], scalar1=math.pi / 2, scalar2=None, op0=ALU.add),
                     reads=[R('ang'), sinres], writes=[R('ang')])
                wrap()
                S.op('act', lambda e: e.activation(out=cos_ap, in_=ang[:], func=AF.Sin), reads=[R('ang')], writes=[cosres])
            return run

        def load_bc(st, name, col0, n=1024):
            t = sbt(st, name, [128, n], F32)
            S.dma(t[:], bcast_rows(mod_s[0:1, col0:col0 + n], 128), 'bc_' + name, reads=['mod_s'], writes=[name])
            return t

        def load_row_bc(st, name, col0, n=1024):
            t = sbt(st, name, [128, n], F32)
            S.dma(t[:], bcast_rows(rows_d[0:1, col0:col0 + n], 128), 'bc_' + name, writes=[name])
            return t

        with ExitStack() as ph:
            ccol = sbt(ph, "ccol", [128, 8], F32)
            cact = sbt(ph, "cact", [128, 8], F32)
            brow = sbt(ph, "brow", [1, 6144], F32)
            mrow = sbt(ph, "mrow", [1, 6144], F32)
            wa = [sbt(ph, "wa%d" % i, [128, 8, 1024], F32) for i in range(2)]
            S.dma(ccol[:], c_d, 'p0a', writes=['ccol'])
            S.dma(brow[:], rows_d[0:1, 0:6144], 'p0a', writes=['brow'])
            S.op('act', lambda e: e.activation(out=cact[:], in_=ccol[:], func=AF.Silu), reads=['ccol'], writes=['cact'])
            for piece in range(6):
                w = wa[piece % 2]
                wres = 'wa%d' % (piece % 2)
                S.dma(w[:], wada_d[:, :, piece * 1024:(piece + 1) * 1024], wres, writes=[wres])
                for half in range(2):
                    bank = piece * 2 + half
                    b = pb[bank % 4]
                    for k in range(8):
                        S.op('pe', lambda e, b=b, w=w, k=k, half=half: e.matmul(
                            b[0:1, :], lhsT=cact[:, k:k + 1], rhs=w[:, k, half * 512:(half + 1) * 512], start=(k == 0), stop=(k == 7)),
                            reads=['cact', wres], writes=[PB(bank % 4)], inc=(k == 7))
                    c0 = bank * 512
                    S.op('dve', lambda e, b=b, c0=c0: e.tensor_tensor(out=mrow[0:1, c0:c0 + 512], in0=b[0:1, :], in1=brow[0:1, c0:c0 + 512], op=ALU.add),
                         reads=[PB(bank % 4), 'brow'], writes=['mrow'])
            S.dma(mod_s, mrow[:], 'mods', reads=['mrow'], writes=['mod_s'])
            dbg_out("mod", mrow[:], [1, 6144], F32, ['mrow'])
            S.barrier()
        if upto <= 0:
            S.barrier()
            S.emit()
            return nc, dbg_outs

        def make_gm(st, name, scale_col0, g_col0, gbuf, gres):
            gm = load_bc(st, name, scale_col0)
            S.dma(gbuf[:], bcast_rows(rows_d[0:1, g_col0:g_col0 + 1024], 128), 'bc_g_' + name, writes=[gres])
            S.op('dve', lambda e: e.scalar_tensor_tensor(out=gm[:], in0=gm[:], scalar=1.0, in1=gbuf[:], op0=ALU.add, op1=ALU.mult),
                 reads=[name, gres], writes=[name])
            return gm

        def norm_to_featmajor(xin_ap, xres, gm, gmres, sh, shres, tmp, tmpres, junk, junkres, hb, hbres, ss, ssres, pbank, dst, dstres):
            S.op('act', lambda e: e.activation(out=junk[:], in_=xin_ap, func=AF.Square, accum_out=ss[:, 0:1]),
                 reads=[xres], writes=[junkres, ssres])
            S.op('act', lambda e: e.activation(out=ss[:, 1:2], in_=ss[:, 0:1], func=AF.Sqrt, scale=1.0 / D, bias=cols[:, 127:128]),
                 reads=[ssres, 'cols'], writes=[ssres])
            S.op('dve', lambda e: e.reciprocal(out=ss[:, 2:3], in_=ss[:, 1:2]), reads=[ssres], writes=[ssres])
            S.op('dve', lambda e: e.scalar_tensor_tensor(out=tmp[:], in0=xin_ap, scalar=ss[:, 2:3], in1=gm[:], op0=ALU.mult, op1=ALU.mult),
                 reads=[xres, ssres, gmres], writes=[tmpres])
            S.op('pool', lambda e: e.tensor_tensor(out=hb[:], in0=tmp[:], in1=sh[:], op=ALU.add),
                 reads=[tmpres, shres], writes=[hbres])
            pbv = pb[pbank][:].bitcast(BF16)
            for k in range(8):
                S.op('pe', lambda e, k=k: e.transpose(out=pbv[:, k * 128:(k + 1) * 128], in_=hb[:, k * 128:(k + 1) * 128], identity=identb[:]),
                     reads=[hbres, 'identb'], writes=[PB(pbank)], inc=(k == 7))
            S.op('act', lambda e: e.copy(out=dst, in_=pbv.rearrange("p (k t) -> p k t", k=8)), reads=[PB(pbank)], writes=[dstres])

        T = 256
        NT = S_LEN // T
        NBLK = T // 128

        with ExitStack() as ph:
            win = sbt(ph, "win", [128, 8, NP_IN], BF16)
            wga = sbt(ph, "wga", [128, 10, 128], BF16)
            wgx = sbt(ph, "wgx", [128, 10, 128], BF16)
            load_weights([(f2(win[:]), f2(win_d), 'win'), (f2(wga[:]), f2(wga_d), 'wga'), (f2(wgx[:]), f2(wgx_d), 'wgx')])
            S.op('dve', lambda e: e.tensor_scalar(out=win[:, :, C_RR0 + 64:C_RR0 + 80], in0=win[:, :, C_RR0 + 64:C_RR0 + 80],
                                                  scalar1=-1.0, scalar2=None, op0=ALU.mult), reads=['win'], writes=['win'])
            tmpn = sbt(ph, "tmpn", [128, 1024], F32)
            gm1 = make_gm(ph, "gm1", 1024, 6144, tmpn, "tmpn")
            sh1 = load_bc(ph, "sh1", 0)
            onesq = sbt(ph, "onesq", [128, 128], F32)
            oneskv = sbt(ph, "oneskv", [128, 128], F32)
            S.op('pool', lambda e: e.memset(onesq[:], 1.0 / 384), writes=['onesq'])
            S.op('pool', lambda e: e.memset(oneskv[:], 1.0 / 256), writes=['oneskv'])
            cL = sbt(ph, "cL", [128, 10], F32)
            S.op('act', lambda e: e.activation(out=cL[:], in_=cols[:, CO_LRU:CO_LRU + 10], func=AF.Sigmoid), reads=['cols'], writes=['cL'])
            S.op('act', lambda e: e.activation(out=cL[:], in_=cL[:], func=AF.Ln), reads=['cL'], writes=['cL'])
            S.op('dve', lambda e: e.tensor_scalar(out=cL[:], in0=cL[:], scalar1=8.0, scalar2=None, op0=ALU.mult), reads=['cL'], writes=['cL'])

            G = 5
            xin = [sbt(ph, "xin%d" % i, [128, 1024], F32) for i in range(2)]
            xring = Ring(range(2))
            junk = sbt(ph, "junk", [128, 1024], BF16)
            hb = [sbt(ph, "hb%d" % i, [128, 1024], BF16) for i in range(2)]
            ssn = [sbt(ph, "ssn%d" % i, [128, 4], F32) for i in range(2)]
            hT = [sbt(ph, "hT%d" % i, [128, 8, T], BF16) for i in range(2)]
            carry = sbt(ph, "carry", [128, 10, 4], F32)
            hcar = sbt(ph, "hcar", [128, 10], F32)
            S.op('pool', lambda e: e.memset(carry[:], 0.0), writes=[('carry', c) for c in range(10)])
            S.op('pool', lambda e: e.memset(hcar[:], 0.0), writes=['hcar'])
            xsS = [sbt(ph, "xsS%d" % i, [128, G, T + 4], F32) for i in range(2)]
            xcS = [sbt(ph, "xcS%d" % i, [128, G, T], F32) for i in range(2)]
            xcbS = [sbt(ph, "xcbS%d" % i, [128, G, T], BF16) for i in range(2)]
            rrS = [sbt(ph, "rrS%d" % i, [128, G, T], F32) for i in range(2)]
            igS = [sbt(ph, "igS%d" % i, [128, G, T], F32) for i in range(2)]
            muS = [sbt(ph, "muS%d" % i, [128, G, T], F32) for i in range(2)]
            hsS = [sbt(ph, "hsS%d" % i, [128, G, T], F32) for i in range(2)]
            yb = sbt(ph, "yb", [128, 10, T], BF16)
            posb = sbt(ph, "posb", [128, T], I32)
            notm = sbt(ph, "notm", [128, T], F32)
            sgr = sbt(ph, "sgr", [128, 8, T], BF16)
            sgt = sbt(ph, "sgt", [128, 8, T], BF16)
            ql = sbt(ph, "ql", [128, 3, T], F32)
            sq = sbt(ph, "sq", [128, 3, T], F32)
            rs = sbt(ph, "rs", [128, T], F32)
            qnt = sbt(ph, "qnt", [128, 3, T], BF16)
            kvnt = sbt(ph, "kvnt", [128, 2, T], BF16)
            krt = sbt(ph, "krt", [128, T], BF16)
            cos_t = sbt(ph, "cos_t", [96, T], F32)
            sin_t = sbt(ph, "sin_t", [96, T], F32)
            t1 = sbt(ph, "t1", [96, T], F32)
            t2 = sbt(ph, "t2", [96, T], F32)
            S.op('pool', lambda e: e.memset(krt[:], 0.0), writes=['krt'])
            cs_run = make_cs_runner(ph, T, 'cs1')

            slotring = Ring([(bk, 0) for bk in range(2, 8)])

            def SL(sl):
                return pb[sl[0]][:, sl[1] * T:(sl[1] + 1) * T]

            def SLR(sl):
                return ('pb', sl[0])

            def proj_mm(sl, col0, M, hres, hTt):
                for k in range(8):
                    S.op('pe', lambda e, k=k: e.matmul(SL(sl)[0:M, :], lhsT=win[:, k, col0:col0 + M], rhs=hTt[:, k, :],
                                                        start=(k == 0), stop=(k == 7)),
                         reads=['win', hres], writes=[SLR(sl)], inc=(k == 7))

            def bc_mid(ap2d, g):
                return bass.AP(ap2d.tensor, ap2d.offset, [list(ap2d.ap[0]), [0, g], list(ap2d.ap[1])])

            def emit_norm(i):
                tok0 = i * T
                slot = i % 2
                for blk in range(NBLK):
                    xi = xring.next()
                    r0 = tok0 + blk * 128
                    S.dma(xin[xi][:], x_d[r0:r0 + 128, :], 'xin%d' % xi, writes=['xin%d' % xi])
                    nb = blk % 2
                    norm_to_featmajor(xin[xi][:], 'xin%d' % xi, gm1, 'gm1', sh1, 'sh1', tmpn, 'tmpn', junk, 'junk', hb[nb], 'hb%d' % nb,
                                      ssn[nb], 'ssn%d' % nb, nb, hT[slot][:, :, blk * 128:(blk + 1) * 128], 'hT%d' % slot)

            def phase1_tile(i):
                tok0 = i * T
                slot = i % 2
                hTt = hT[slot]
                hres = 'hT%d' % slot
                XS = lambda st, g: ('xs', st, g)
                XC = lambda st, g: ('xc', st, g)
                XCB = lambda st, g: ('xcb', st, g)
                RR = lambda st, g: ('rr', st, g)
                IG = lambda st, g: ('ig', st, g)
                MU = lambda st, g: ('mu', st, g)
                HS = lambda st, g: ('hs', st, g)
                allg = lambda f, st: [f(st, g) for g in range(G)]
                S.dma(posb[:], bcast_rows(pos_d[0:1, tok0:tok0 + T], 128), 'posb', writes=['posb'])
                S.op('dve', lambda e: e.tensor_copy(out=notm[:], in_=posb[:]), reads=['posb'], writes=['notm'])
                S.op('dve', lambda e: e.tensor_scalar(out=notm[:], in0=notm[:], scalar1=0.0, scalar2=None, op0=ALU.not_equal),
                     reads=['notm'], writes=['notm'])
                for st in range(2):
                    for g in range(G):
                        c = G * st + g
                        psl = slotring.next()
                        proj_mm(psl, C_RNN0 + c * 128, 128, hres, hTt)
                        S.op('pool', lambda e, st=st, g=g, c=c: e.tensor_copy(out=xsS[st][:, g, 1:4], in_=carry[:, c, 1:4]),
                             reads=[('carry', c)], writes=[XS(st, g)])
                        S.op('act', lambda e, st=st, g=g, psl=psl: e.copy(out=xsS[st][:, g, 4:T + 4], in_=SL(psl)),
                             reads=[SLR(psl)], writes=[XS(st, g)])
                        S.op('pool', lambda e, st=st, g=g, c=c: e.tensor_copy(out=carry[:, c, 1:4], in_=xsS[st][:, g, T + 1:T + 4]),
                             reads=[XS(st, g)], writes=[('carry', c)])
                for (col0g, dstt, dres, dram, semk) in ((C_GM0, sgt, 'sgt', sgm_s, 'sgt'), (C_GR0, sgr, 'sgr', sgr_s, 'sgr')):
                    for oc in range(8):
                        gb = slotring.next()
                        proj_mm(gb, col0g + oc * 128, 128, hres, hTt)
                        S.op('act', lambda e, gb=gb, oc=oc, dstt=dstt: e.activation(out=dstt[:, oc, :], in_=SL(gb), func=AF.Sigmoid),
                             reads=[SLR(gb)], writes=[dres])
                    S.dma(dram.rearrange("(o p) t -> p o t", p=128)[:, :, tok0:tok0 + T], dstt[:], semk, reads=[dres], writes=[semk + '_d'])
                for st in range(2):
                    for g in range(G):
                        c = G * st + g
                        cw = lambda k, c=c: cols[:, CO_CONVW + c * 4 + k:CO_CONVW + c * 4 + k + 1]
                        S.op('dve', lambda e, st=st, g=g, c=c, cw=cw: e.tensor_scalar(out=xcS[st][:, g, :], in0=xsS[st][:, g, 1:T + 1], scalar1=cw(0),
                                                                                      scalar2=cols[:, CO_CONVB + c:CO_CONVB + c + 1], op0=ALU.mult, op1=ALU.add),
                             reads=[XS(st, g), 'cols'], writes=[XC(st, g)])
                        for k in range(1, 4):
                            S.op('dve', lambda e, st=st, g=g, k=k, cw=cw: e.scalar_tensor_tensor(out=xcS[st][:, g, :], in0=xsS[st][:, g, 1 + k:T + 1 + k], scalar=cw(k),
                                                                                                 in1=xcS[st][:, g, :], op0=ALU.mult, op1=ALU.add),
                                 reads=[XS(st, g), 'cols', XC(st, g)], writes=[XC(st, g)])
                    S.op('pool', lambda e, st=st: e.tensor_copy(out=xcbS[st][:].rearrange("p g t -> p (g t)"), in_=xcS[st][:].rearrange("p g t -> p (g t)")), reads=allg(XC, st), writes=allg(XCB, st))
                for st in range(2):
                    for g in range(G):
                        c = G * st + g
                        sa = slotring.next()
                        S.op('pe', lambda e, c=c, st=st, g=g, sa=sa: e.matmul(SL(sa), lhsT=wga[:, c, :], rhs=xcbS[st][:, g, :], start=True, stop=True),
                             reads=['wga', XCB(st, g)], writes=[SLR(sa)])
                        sx = slotring.next()
                        S.op('pe', lambda e, c=c, st=st, g=g, sx=sx: e.matmul(SL(sx), lhsT=wgx[:, c, :], rhs=xcbS[st][:, g, :], start=True, stop=True),
                             reads=['wgx', XCB(st, g)], writes=[SLR(sx)])
                        S.op('act', lambda e, c=c, st=st, g=g, sa=sa: e.activation(out=rrS[st][:, g, :], in_=SL(sa), func=AF.Sigmoid, bias=cols[:, CO_BA + c:CO_BA + c + 1]),
                             reads=[SLR(sa), 'cols'], writes=[RR(st, g)])
                        S.op('act', lambda e, c=c, st=st, g=g, sx=sx: e.activation(out=igS[st][:, g, :], in_=SL(sx), func=AF.Sigmoid, bias=cols[:, CO_BX + c:CO_BX + c + 1]),
                             reads=[SLR(sx), 'cols'], writes=[IG(st, g)])
                        S.op('act', lambda e, c=c, st=st, g=g: e.activation(out=rrS[st][:, g, :], in_=rrS[st][:, g, :], func=AF.Exp, scale=cL[:, c:c + 1]),
                             reads=[RR(st, g), 'cL'], writes=[RR(st, g)])
                if i + 1 < ntile1:
                    emit_norm(i + 1)
                for st in range(2):
                    for g in range(G):
                        S.op('dve', lambda e, st=st, g=g: e.tensor_tensor(out=rrS[st][:, g, :], in0=rrS[st][:, g, :], in1=notm[:], op=ALU.mult),
                             reads=[RR(st, g), 'notm'], writes=[RR(st, g)])
                    S.op('act', lambda e, st=st: e.activation(out=muS[st][:].rearrange("p g t -> p (g t)"), in_=rrS[st][:].rearrange("p g t -> p (g t)"), func=AF.Square), reads=allg(RR, st), writes=allg(MU, st))
                    S.op('act', lambda e, st=st: e.activation(out=muS[st][:].rearrange("p g t -> p (g t)"), in_=muS[st][:].rearrange("p g t -> p (g t)"), func=AF.Sqrt, scale=-1.0, bias=cols[:, 126:127]),
                         reads=allg(MU, st) + ['cols'], writes=allg(MU, st))
                    S.op('pool', lambda e, st=st: e.tensor_tensor(out=igS[st][:].rearrange("p g t -> p (g t)"), in0=igS[st][:].rearrange("p g t -> p (g t)"), in1=xcS[st][:].rearrange("p g t -> p (g t)"), op=ALU.mult),
                         reads=allg(IG, st) + allg(XC, st), writes=allg(IG, st))
                    S.op('dve', lambda e, st=st: e.tensor_tensor(out=igS[st][:].rearrange("p g t -> p (g t)"), in0=igS[st][:].rearrange("p g t -> p (g t)"), in1=muS[st][:].rearrange("p g t -> p (g t)"), op=ALU.mult),
                         reads=allg(IG, st) + allg(MU, st), writes=allg(IG, st))
                for st in range(2):
                    for g in range(G):
                        c = G * st + g
                        S.op('dve', lambda e, st=st, g=g, c=c: e.tensor_tensor_scan(out=hsS[st][:, g, :], data0=rrS[st][:, g, :], data1=igS[st][:, g, :],
                                                                                    initial=hcar[:, c:c + 1], op0=ALU.mult, op1=ALU.add),
                             reads=[RR(st, g), IG(st, g), 'hcar'], writes=[HS(st, g)])
                    S.op('dve', lambda e, st=st: e.tensor_copy(out=hcar[:, G * st:G * st + G], in_=hsS[st][:, :, T - 1]), reads=allg(HS, st), writes=['hcar'])
                    S.op('pool', lambda e, st=st: e.tensor_copy(out=yb[:, G * st:G * st + G, :].rearrange("p g t -> p (g t)"), in_=hsS[st][:].rearrange("p g t -> p (g t)")), reads=allg(HS, st), writes=['yb'])
                S.dma(yrnn_s.rearrange("(o p) t -> p o t", p=128)[:, :, tok0:tok0 + T], yb[:], 'ybs', reads=['yb'], writes=['yrnn_s'])
                if dbg and upto <= 1 and i == 1:
                    dbg_out("xc", xcS[0][:, 0, :], [128, T], F32, [XC(0, 0)])
                    dbg_out("hs", hsS[0][:, 0, :], [128, T], F32, [HS(0, 0)])

                def latent(nch, col0, ones_t, onesres, gco, outt, outres, dram, semk):
                    for qc in range(nch):
                        b = slotring.next()
                        proj_mm(b, col0 + qc * 128, 128, hres, hTt)
                        S.op('act', lambda e, b=b, qc=qc: e.copy(out=ql[:, qc, :], in_=SL(b)), reads=[SLR(b)], writes=[('ql', qc)])
                        S.op('dve', lambda e, b=b, qc=qc: e.tensor_tensor(out=sq[:, qc, :], in0=SL(b), in1=ql[:, qc, :], op=ALU.mult),
                             reads=[SLR(b), ('ql', qc)], writes=[('sq', qc)])
                    b = slotring.next()
                    for qc in range(nch):
                        S.op('pe', lambda e, b=b, qc=qc: e.matmul(SL(b), lhsT=ones_t[:], rhs=sq[:, qc, :], start=(qc == 0), stop=(qc == nch - 1)),
                             reads=[onesres, ('sq', qc)], writes=[SLR(b)], inc=(qc == nch - 1))
                    S.op('act', lambda e, b=b: e.activation(out=rs[:], in_=SL(b), func=AF.Sqrt, bias=cols[:, 127:128]),
                         reads=[SLR(b), 'cols'], writes=['rs'])
                    S.op('dve', lambda e: e.reciprocal(out=rs[:], in_=rs[:]), reads=['rs'], writes=['rs'])
                    for qc in range(nch):
                        S.op('dve', lambda e, qc=qc: e.scalar_tensor_tensor(out=outt[:, qc, :], in0=ql[:, qc, :], scalar=cols[:, gco + qc:gco + qc + 1], in1=rs[:],
                                                                            op0=ALU.mult, op1=ALU.mult),
                             reads=[('ql', qc), 'cols', 'rs'], writes=[outres])
                    S.dma(dram.rearrange("(o p) t -> p o t", p=128)[:, :, tok0:tok0 + T], outt[:], semk, reads=[outres], writes=[semk + '_d'])
                latent(3, C_Q0, onesq, 'onesq', CO_GQ, qnt, 'qnt', qn_s, 'qns')
                latent(2, C_KV0, oneskv, 'oneskv', CO_GKV, kvnt, 'kvnt', kvn_s, 'kvns')
                if dbg and upto <= 1 and i == 1:
                    dbg_out("qnt", qnt[:], [128, 3, T], BF16, ['qnt'])
                cs_run(tok0, cos_t[:], sin_t[:], 'cos_t', 'sin_t')
                S.dma(cos_s[:, tok0:tok0 + T], cos_t[:], 'coss', reads=['cos_t'], writes=['cos_s'])
                S.dma(sin_s[:, tok0:tok0 + T], sin_t[:], 'sins', reads=['sin_t'], writes=['sin_s'])
                bm = slotring.next()
                proj_mm(bm, C_RM0, 96, hres, hTt)
                S.op('dve', lambda e, bm=bm: e.tensor_tensor(out=t1[64:96, :], in0=SL(bm)[64:96, :], in1=cos_t[64:96, :], op=ALU.mult),
                     reads=[SLR(bm), 'cos_t'], writes=['t1'])
                br = slotring.next()
                proj_mm(br, C_RR0, 96, hres, hTt)
                S.op('dve', lambda e, br=br: e.tensor_tensor(out=t2[64:96, :], in0=SL(br)[64:96, :], in1=sin_t[64:96, :], op=ALU.mult),
                     reads=[SLR(br), 'sin_t'], writes=['t2'])
                S.op('dve', lambda e: e.tensor_tensor(out=krt[64:96, :], in0=t1[64:96, :], in1=t2[64:96, :], op=ALU.add),
                     reads=['t1', 't2'], writes=['krt'])
                S.dma(kr_s[:, tok0:tok0 + T], krt[:], 'krs', reads=['krt'], writes=['kr_s'])

            ntile1 = 2 if (dbg and upto <= 1) else NT
            emit_norm(0)
            for i in range(ntile1):
                phase1_tile(i)
            if dbg and upto <= 1:
                dbg_out("hT", hT[1][:], [128, 8, T], BF16, ['hT1'])
                dbg_out("krt", krt[:], [128, T], BF16, ['krt'])
                dbg_out("sgt", sgt[:], [128, 8, T], BF16, ['sgt'])
                dbg_out("kvnt", kvnt[:], [128, 2, T], BF16, ['kvnt'])
                dbg_out("cos", cos_t[:], [96, T], F32, ['cos_t'])
                dbg_out("sin", sin_t[:], [96, T], F32, ['sin_t'])
            S.barrier()
        if upto <= 1:
            S.barrier()
            S.emit()
            return nc, dbg_outs

        with ExitStack() as ph:
            T2 = 512
            wqm = sbt(ph, "wqm", [128, 3, 1536], BF16)
            wqr = sbt(ph, "wqr", [128, 3, 1536], BF16)
            wk = sbt(ph, "wk", [128, 2, 1024], BF16)
            wv = sbt(ph, "wv", [128, 2, 1024], BF16)
            qn = sbt(ph, "qn", [128, 3, S_LEN], BF16)
            kvn = sbt(ph, "kvn", [128, 2, S_LEN], BF16)
            kr = sbt(ph, "kr", [128, S_LEN], BF16)
            S.dma(qn[:], qn_s.rearrange("(o p) t -> p o t", p=128), 'p2l', reads=['qns_d'], writes=['qn'])
            S.dma(kvn[:], kvn_s.rearrange("(o p) t -> p o t", p=128), 'p2l', reads=['kvns_d'], writes=['kvn'])
            S.dma(kr[:], kr_s, 'p2l', reads=['kr_s'], writes=['kr'])
            load_weights([(f2(wqm[:]), f2(wuqm_d), 'wqm'), (f2(wqr[:]), f2(wuqr_d), 'wqr'), (f2(wk[:]), f2(wk_d), 'wk'), (f2(wv[:]), f2(wv_d), 'wv')])
            for k in range(3):
                v4 = wqr[:, k, :].rearrange("p (h d) -> p h d", d=96)[:, :, 64:80]
                S.op('dve', lambda e, v4=v4: e.tensor_scalar(out=v4, in0=v4, scalar1=-1.0, scalar2=None, op0=ALU.mult), reads=['wqr'], writes=['wqr'])
            cosf = sbt(ph, "cosf", [96, S_LEN], F32)
            sinf = sbt(ph, "sinf", [96, S_LEN], F32)
            S.dma(cosf[:], cos_s, 'p2l', reads=['cos_s'], writes=[('cosf', i) for i in range(S_LEN // T2)])
            S.dma(sinf[:], sin_s, 'p2l', reads=['sin_s'], writes=[('sinf', i) for i in range(S_LEN // T2)])
            QT = [sbt(ph, "QT%d" % i, [96, S_LEN], BF16) for i in range(2)]
            KT = [sbt(ph, "KT%d" % i, [96, S_LEN], BF16) for i in range(2)]
            VV = [sbt(ph, "VV%d" % i, [128, 32, 65], BF16) for i in range(2)]
            y4 = [sbt(ph, "y4_%d" % i, [128, 32, 256], BF16) for i in range(2)]
            pT = [sbt(ph, "pT%d" % i, [128, 512], BF16) for i in range(3)]
            rec = [sbt(ph, "rec%d" % i, [128, 4], F32) for i in range(2)]
            tq1 = sbt(ph, "tq1", [96, T2], F32)
            tq2 = sbt(ph, "tq2", [96, T2], F32)
            for par in range(2):
                S.op('pool', lambda e, par=par: e.memset(VV[par][:, :, 64:65], 1.0), writes=[('V', par, i) for i in range(8)])
            sring = Ring([0, 1, 2])
            oring = Ring([3, 4])
            pring = Ring([5, 6, 7])
            ptring = Ring([0, 1, 2])
            SCALE = 1.0 / math.sqrt(96.0)

            def head_proj(h):
                par = h % 2
                for i in range(S_LEN // T2):
                    c0 = i * T2
                    bq = pring.next()
                    for k in range(3):
                        S.op('pe', lambda e, k=k, bq=bq: e.matmul(pb[bq][0:96, :], lhsT=wqm[:, k, h * 96:(h + 1) * 96], rhs=qn[:, k, c0:c0 + T2],
                                                                   start=(k == 0), stop=(k == 2)), reads=['wqm', 'qn'], writes=[PB(bq)], inc=(k == 2))
                    S.op('dve', lambda e, bq=bq: e.tensor_copy(out=QT[par][0:64, c0:c0 + T2], in_=pb[bq][0:64, :]), reads=[PB(bq)], writes=[('QT', par, i)])
                    S.op('dve', lambda e, bq=bq: e.tensor_tensor(out=tq1[64:96, :], in0=pb[bq][64:96, :], in1=cosf[64:96, c0:c0 + T2], op=ALU.mult),
                         reads=[PB(bq), ('cosf', i)], writes=['tq1'])
                    br_ = pring.next()
                    for k in range(3):
                        S.op('pe', lambda e, k=k, br_=br_: e.matmul(pb[br_][0:96, :], lhsT=wqr[:, k, h * 96:(h + 1) * 96], rhs=qn[:, k, c0:c0 + T2],
                                                                     start=(k == 0), stop=(k == 2)), reads=['wqr', 'qn'], writes=[PB(br_)], inc=(k == 2))
                    S.op('dve', lambda e, br_=br_: e.tensor_tensor(out=tq2[64:96, :], in0=pb[br_][64:96, :], in1=sinf[64:96, c0:c0 + T2], op=ALU.mult),
                         reads=[PB(br_), ('sinf', i)], writes=['tq2'])
                    S.op('dve', lambda e: e.tensor_tensor(out=QT[par][64:96, c0:c0 + T2], in0=tq1[64:96, :], in1=tq2[64:96, :], op=ALU.add),
                         reads=['tq1', 'tq2'], writes=[('QT', par, i)])
                    bk = pring.next()
                    for k in range(2):
                        S.op('pe', lambda e, k=k, bk=bk: e.matmul(pb[bk][0:64, :], lhsT=wk[:, k, h * 64:(h + 1) * 64], rhs=kvn[:, k, c0:c0 + T2],
                                                                   start=(k == 0), stop=(k == 1)), reads=['wk', 'kvn'], writes=[PB(bk)], inc=(k == 1))
                    S.op('dve', lambda e, bk=bk: e.tensor_copy(out=KT[par][0:64, c0:c0 + T2], in_=pb[bk][0:64, :]), reads=[PB(bk)], writes=[('KT', par, i)])
                    S.op('pool', lambda e: e.tensor_copy(out=KT[par][64:96, c0:c0 + T2], in_=kr[64:96, c0:c0 + T2]), reads=['kr'], writes=[('KT', par, i)])
                    bv = pring.next()
                    for blk in range(4):
                        for k in range(2):
                            S.op('pe', lambda e, k=k, blk=blk, bv=bv: e.matmul(pb[bv][:, blk * 64:(blk + 1) * 64], lhsT=kvn[:, k, c0 + blk * 128:c0 + (blk + 1) * 128],
                                                                                rhs=wv[:, k, h * 64:(h + 1) * 64], start=(k == 0), stop=(k == 1)),
                                 reads=['wv', 'kvn'], writes=[PB(bv)], inc=(k == 1 and blk == 3))
                    S.op('dve', lambda e, bv=bv: e.tensor_copy(out=VV[par][:, i * 4:(i + 1) * 4, 0:64], in_=pb[bv][:, 0:256].rearrange("p (b d) -> p b d", d=64)),
                         reads=[PB(bv)], writes=[('V', par, i)])

            def head_attn(h):
                par = h % 2
                g = h // 4
                hl = h % 4
                yb4 = y4[g % 2]
                y4res = 'y4_%d' % (g % 2)
                for i in range(S_LEN // T2):
                    ob = oring.next()
                    nj = 4 * i + 4
                    state = {'first': True}

                    def emit_pv(j, pt, qlo):
                        qbs = list(range(qlo // 128, 4))
                        for qb in qbs:
                            st_ = state['first']
                            state['first'] = False
                            S.op('pe', lambda e, qb=qb, st_=st_: e.matmul(pb[ob][:, qb * 128:qb * 128 + 65], lhsT=pT[pt][:, qb * 128:(qb + 1) * 128], rhs=VV[par][:, j, :],
                                                                            start=st_, stop=(j == 4 * i + qb), skip_group_check=True),
                                 reads=[('pT', pt), ('V', par, j // 4)], writes=[PB(ob)], inc=(qb == qbs[-1]))
                    prev = None
                    for j in range(nj):
                        qlo = max(0, j - 4 * i) * 128
                        sb_ = sring.next()
                        S.op('pe', lambda e, j=j, qlo=qlo, sb_=sb_: e.matmul(pb[sb_][:, qlo:512], lhsT=KT[par][:, j * 128:(j + 1) * 128],
                                                                               rhs=QT[par][:, i * 512 + qlo:(i + 1) * 512], start=True, stop=True),
                             reads=[('KT', par, j // 4), ('QT', par, i)], writes=[PB(sb_)])
                        pt = ptring.next()
                        S.op('act', lambda e, qlo=qlo, sb_=sb_, pt=pt: e.activation(out=pT[pt][:, qlo:512], in_=pb[sb_][:, qlo:512], func=AF.Exp, scale=SCALE),
                             reads=[PB(sb_)], writes=[('pT', pt)])
                        if j >= 4 * i:
                            S.op('pool', lambda e, qlo=qlo, pt=pt: e.affine_select(out=pT[pt][:, qlo:qlo + 128], in_=pT[pt][:, qlo:qlo + 128], pattern=[[1, 128]],
                                                                                    compare_op=ALU.is_ge, fill=0.0, base=0, channel_multiplier=-1),
                                 reads=[('pT', pt)], writes=[('pT', pt)])
                        if prev is not None:
                            emit_pv(*prev)
                        prev = (j, pt, qlo)
                    emit_pv(*prev)
                    rc = rec[i % 2]
                    rcres = 'rec%d' % (i % 2)
                    S.op('dve', lambda e, rc=rc: e.reciprocal(out=rc[:, 0:4], in_=pb[ob][:, :].rearrange("p (q d) -> p q d", d=128)[:, :, 64]),
                         reads=[PB(ob)], writes=[rcres])
                    for qb in range(4):
                        S.op('dve', lambda e, qb=qb, rc=rc: e.tensor_scalar(out=yb4[:, 4 * i + qb, hl * 64:(hl + 1) * 64], in0=pb[ob][:, qb * 128:qb * 128 + 64],
                                                                              scalar1=rc[:, qb:qb + 1], scalar2=None, op0=ALU.mult),
                             reads=[PB(ob), rcres], writes=[y4res])
                if hl == 3:
                    S.dma(ymla_s.rearrange("(b p) f -> p b f", p=128)[:, :, g * 256:(g + 1) * 256], yb4[:], 'y4s%d' % (g % 2), reads=[y4res], writes=['ymla_s'])

            nheads = NH
            head_proj(0)
            for h in range(nheads):
                if h + 1 < nheads:
                    head_proj(h + 1)
                head_attn(h)
            S.barrier()
        if dbg:
            t = nc.dram_tensor("dbg2_y_mla", [S_LEN, D], BF16, kind="ExternalOutput").ap()
            S.dma(t, ymla_s, 'dbg2y', reads=['ymla_s'], writes=['dbg2y'])
        if upto <= 2:
            S.barrier()
            S.emit()
            return nc, dbg_outs

        with ExitStack() as ph:
            T3 = 512
            wpm = sbt(ph, "wpm", [128, 8, 1024], BF16)
            wo = sbt(ph, "wo", [128, 8, 1024], BF16)
            wpr = sbt(ph, "wpr", [128, 10, 1024], BF16)
            load_weights([(f2(wpm[:]), f2(wpm_d), 'wpm'), (f2(wo[:]), f2(wo_d), 'wo'), (f2(wpr[:]), f2(wpr_d), 'wpr')])
            yr3 = [sbt(ph, "yr3_%d" % i, [128, 10, T3], BF16) for i in range(2)]
            g1bc = load_bc(ph, "g1bc", 2048)
            ytok = [sbt(ph, "ytok%d" % i, [128, 4, 1024], BF16) for i in range(2)]
            sgt3 = [sbt(ph, "sgt3_%d" % i, [128, 8, T3], BF16) for i in range(2)]
            mrt3 = [sbt(ph, "mrt3_%d" % i, [128, 8, T3], BF16) for i in range(2)]
            yT = sbt(ph, "yT", [128, 8, T3], BF16)
            mg = sbt(ph, "mg", [128, 8, T3], BF16)
            tmg = [sbt(ph, "tmg%d" % i, [128, T3], F32) for i in range(2)]
            tmg2 = [sbt(ph, "tmg2_%d" % i, [128, T3], F32) for i in range(2)]
            xin3 = [sbt(ph, "xin3_%d" % i, [128, 1024], F32) for i in range(2)]
            tmo = [sbt(ph, "tmo%d" % i, [128, 1024], F32) for i in range(2)]
            mm3 = Ring([2, 3, 4, 5, 6, 7])
            xr3 = Ring([0, 1])
            for i in range(S_LEN // T3):
                c0 = i * T3
                sl = i % 2
                S.dma(ytok[sl][:], ymla_s.rearrange("(b p) f -> p b f", p=128)[:, 4 * i:4 * i + 4, :], 'yt%d' % sl, reads=['ymla_s'], writes=['ytok%d' % sl])
                S.dma(sgt3[sl][:], sgm_s.rearrange("(o p) t -> p o t", p=128)[:, :, c0:c0 + T3], 'sg3%d' % sl, reads=['sgm_s'], writes=['sgt3_%d' % sl])
                S.dma(mrt3[sl][:], sgr_s.rearrange("(o p) t -> p o t", p=128)[:, :, c0:c0 + T3], 'mr3%d' % sl, reads=['sgr_d'], writes=['mrt3_%d' % sl])
                S.dma(yr3[sl][:], yrnn_s.rearrange("(o p) t -> p o t", p=128)[:, :, c0:c0 + T3], 'yr3%d' % sl, reads=['yrnn_s'], writes=['yr3_%d' % sl])
                for blk in range(4):
                    tb = blk % 2
                    pbv = pb[tb][:].bitcast(BF16)
                    for k in range(8):
                        S.op('pe', lambda e, k=k, blk=blk, pbv=pbv: e.transpose(out=pbv[:, k * 128:(k + 1) * 128], in_=ytok[sl][:, blk, k * 128:(k + 1) * 128], identity=identb[:]),
                             reads=['ytok%d' % sl, 'identb'], writes=[PB(tb)], inc=(k == 7))
                    S.op('act', lambda e, blk=blk, pbv=pbv: e.copy(out=yT[:, :, blk * 128:(blk + 1) * 128], in_=pbv.rearrange("p (k t) -> p k t", k=8)),
                         reads=[PB(tb)], writes=['yT'])
                for oc in range(8):
                    b = mm3.next()
                    for k in range(8):
                        S.op('pe', lambda e, k=k, oc=oc, b=b: e.matmul(pb[b][:, :], lhsT=wpm[:, k, oc * 128:(oc + 1) * 128], rhs=yT[:, k, :], start=(k == 0), stop=(k == 7)),
                             reads=['wpm', 'yT'], writes=[PB(b)], inc=(k == 7))
                    tm = tmg[oc % 2]
                    S.op('dve', lambda e, oc=oc, b=b, tm=tm: e.tensor_tensor(out=tm[:], in0=pb[b][:, :], in1=sgt3[sl][:, oc, :], op=ALU.mult),
                         reads=[PB(b), 'sgt3_%d' % sl], writes=['tmg%d' % (oc % 2)])
                    b2 = mm3.next()
                    for cc in range(10):
                        S.op('pe', lambda e, cc=cc, oc=oc, b2=b2: e.matmul(pb[b2][:, :], lhsT=wpr[:, cc, oc * 128:(oc + 1) * 128], rhs=yr3[sl][:, cc, :], start=(cc == 0), stop=(cc == 9)),
                             reads=['wpr', 'yr3_%d' % sl], writes=[PB(b2)], inc=(cc == 9))
                    tm2 = tmg2[oc % 2]
                    S.op('dve', lambda e, oc=oc, b2=b2, tm2=tm2: e.tensor_tensor(out=tm2[:], in0=pb[b2][:, :], in1=mrt3[sl][:, oc, :], op=ALU.mult),
                         reads=[PB(b2), 'mrt3_%d' % sl], writes=['tmg2_%d' % (oc % 2)])
                    S.op('pool', lambda e, oc=oc, tm=tm, tm2=tm2: e.tensor_tensor(out=mg[:, oc, :], in0=tm[:], in1=tm2[:], op=ALU.add),
                         reads=['tmg%d' % (oc % 2), 'tmg2_%d' % (oc % 2)], writes=['mg'])
                for blk in range(4):
                    xi = xr3.next()
                    r0 = c0 + blk * 128
                    S.dma(xin3[xi][:], x_d[r0:r0 + 128, :], 'xin3_%d' % xi, writes=['xin3_%d' % xi])
                    to = tmo[xi]
                    for half in range(2):
                        b = mm3.next()
                        for k in range(8):
                            S.op('pe', lambda e, k=k, blk=blk, half=half, b=b: e.matmul(pb[b][:, :], lhsT=mg[:, k, blk * 128:(blk + 1) * 128], rhs=wo[:, k, half * 512:(half + 1) * 512],
                                                                                       start=(k == 0), stop=(k == 7)),
                                 reads=['wo', 'mg'], writes=[PB(b)], inc=(k == 7))
                        S.op('dve', lambda e, half=half, b=b, to=to: e.tensor_tensor(out=to[:, half * 512:(half + 1) * 512], in0=pb[b][:, :], in1=g1bc[:, half * 512:(half + 1) * 512], op=ALU.mult),
                             reads=[PB(b), 'g1bc'], writes=['tmo%d' % xi])
                    S.op('pool', lambda e, xi=xi, to=to: e.tensor_tensor(out=to[:], in0=to[:], in1=xin3[xi][:], op=ALU.add),
                         reads=['tmo%d' % xi, 'xin3_%d' % xi], writes=['tmo%d' % xi])
                    S.dma(x1_s[r0:r0 + 128, :], to[:], 'x1s%d' % xi, reads=['tmo%d' % xi], writes=['x1_s'])
            S.barrier()
        if dbg:
            t = nc.dram_tensor("dbg2_x1", [S_LEN, D], F32, kind="ExternalOutput").ap()
            S.dma(t, x1_s, 'dbg2x', reads=['x1_s'], writes=['dbg2x'])
        if upto <= 3:
            S.barrier()
            S.emit()
            return nc, dbg_outs

        with ExitStack() as ph:
            T4 = 256
            NB4 = T4 // 128
            wup = sbt(ph, "wup", [128, 8, 2 * D_FF], BF16)
            wdn = sbt(ph, "wdn", [128, 22, 1024], BF16)
            load_weights([(f2(wup[:]), f2(wup_d), 'wup'), (f2(wdn[:]), f2(wdn_d), 'wdn')])
            tmpo = sbt(ph, "tmpo", [128, 1024], F32)
            gm2 = make_gm(ph, "gm2", 4096, 6144 + 1024, tmpo, 'tmpo')
            sh2 = load_bc(ph, "sh2", 3072)
            g2bc = load_bc(ph, "g2bc", 5120)
            fgbc = load_row_bc(ph, "fgbc", 6144 + 2048)
            fcw = sbt(ph, "fcw", [128, 132], F32)
            fcb = sbt(ph, "fcb", [128, 44], F32)
            S.dma(fcw[:], fcw_d, 'p4c', writes=['fcw'])
            S.dma(fcb[:], fcb_d, 'p4c', writes=['fcb'])
            carry2 = sbt(ph, "carry2", [128, 44, 2], F32)
            S.op('pool', lambda e: e.memset(carry2[:], 0.0), writes=[('carry2', ch) for ch in range(44)])
            x1t = [sbt(ph, "x1t%d" % i, [128, 1024], F32) for i in range(2)]
            junk4 = sbt(ph, "junk4", [128, 1024], BF16)
            hb4 = [sbt(ph, "hb4_%d" % i, [128, 1024], BF16) for i in range(2)]
            ss4 = [sbt(ph, "ss4_%d" % i, [128, 4], F32) for i in range(2)]
            ss5 = [sbt(ph, "ss5_%d" % i, [128, 4], F32) for i in range(2)]
            h2T = [sbt(ph, "h2T%d" % i, [128, 8, T4], BF16) for i in range(2)]
            NXS, NU = 4, 3
            xs2 = [sbt(ph, "xs2_%d" % i, [128, T4 + 2], F32) for i in range(NXS)]
            ug = [sbt(ph, "ug%d" % i, [128, T4], F32) for i in range(NU)]
            uv = [sbt(ph, "uv%d" % i, [128, T4], F32) for i in range(NU)]
            uctr = [0]
            prod = sbt(ph, "prod", [128, 22, T4], BF16)
            mm4 = Ring([2, 3, 4, 5, 6, 7])
            xsr4 = Ring(range(NXS))
            for i in range(S_LEN // T4):
                c0 = i * T4
                hTt = h2T[i % 2]
                hres = 'h2T%d' % (i % 2)
                for blk in range(NB4):
                    r0 = c0 + blk * 128
                    S.dma(x1t[blk][:], x1_s[r0:r0 + 128, :], 'x1t%d' % blk, reads=['x1_s'], writes=['x1t%d' % blk])
                    norm_to_featmajor(x1t[blk][:], 'x1t%d' % blk, gm2, 'gm2', sh2, 'sh2', tmpo, 'tmpo', junk4, 'junk4', hb4[blk], 'hb4_%d' % blk,
                                      ss4[blk], 'ss4_%d' % blk, blk, hTt[:, :, blk * 128:(blk + 1) * 128], hres)
                pend = []

                def flush_pair():
                    (cc, ugb, ugres, uvb, uvres) = pend.pop(0)
                    S.op('act', lambda e: e.activation(out=ugb[:], in_=ugb[:], func=AF.Silu), reads=[ugres], writes=[ugres])
                    S.op('dve', lambda e: e.tensor_tensor(out=prod[:, cc, :], in0=ugb[:], in1=uvb[:], op=ALU.mult),
                         reads=[ugres, uvres], writes=[('prod', cc)])

                for c in range(22):
                    ubuf = {}
                    ui = uctr[0] % NU
                    uctr[0] += 1
                    for part, ch in (('g', c), ('v', c + 22)):
                        b = mm4.next()
                        for k in range(8):
                            S.op('pe', lambda e, k=k, ch=ch, b=b: e.matmul(pb[b][:, 0:T4], lhsT=wup[:, k, ch * 128:(ch + 1) * 128], rhs=hTt[:, k, :], start=(k == 0), stop=(k == 7)),
                                 reads=['wup', hres], writes=[PB(b)], inc=(k == 7))
                        xsi = xsr4.next()
                        xb = xs2[xsi]
                        xres = 'xs2_%d' % xsi
                        S.op('act', lambda e, ch=ch, xb=xb: e.copy(out=xb[:, 0:2], in_=carry2[:, ch, :]), reads=[('carry2', ch)], writes=[xres])
                        S.op('act', lambda e, b=b, xb=xb: e.copy(out=xb[:, 2:T4 + 2], in_=pb[b][:, 0:T4]), reads=[PB(b)], writes=[xres])
                        S.op('act', lambda e, ch=ch, xb=xb: e.copy(out=carry2[:, ch, :], in_=xb[:, T4:T4 + 2]), reads=[xres], writes=[('carry2', ch)])
                        u = (ug if part == 'g' else uv)[ui]
                        ures = ('ug%d' if part == 'g' else 'uv%d') % ui
                        S.op('pool', lambda e, ch=ch, xb=xb, u=u: e.tensor_scalar(out=u[:], in0=xb[:, 0:T4], scalar1=fcw[:, ch * 3:ch * 3 + 1], scalar2=fcb[:, ch:ch + 1],
                                                                                  op0=ALU.mult, op1=ALU.add), reads=[xres, 'fcw', 'fcb'], writes=[ures])
                        for kk in (1, 2):
                            S.op('dve', lambda e, ch=ch, xb=xb, u=u, kk=kk: e.scalar_tensor_tensor(out=u[:], in0=xb[:, kk:T4 + kk], scalar=fcw[:, ch * 3 + kk:ch * 3 + kk + 1], in1=u[:],
                                                                                                  op0=ALU.mult, op1=ALU.add), reads=[xres, 'fcw', ures], writes=[ures])
                        ubuf[part] = (u, ures)
                    if pend:
                        flush_pair()
                    pend.append((c, ubuf['g'][0], ubuf['g'][1], ubuf['v'][0], ubuf['v'][1]))
                while pend:
                    flush_pair()
                for blk in range(NB4):
                    r0 = c0 + blk * 128
                    for half in range(2):
                        b = mm4.next()
                        for c in range(22):
                            S.op('pe', lambda e, c=c, blk=blk, half=half, b=b: e.matmul(pb[b][:, :], lhsT=prod[:, c, blk * 128:(blk + 1) * 128], rhs=wdn[:, c, half * 512:(half + 1) * 512],
                                                                                       start=(c == 0), stop=(c == 21)),
                                 reads=['wdn', ('prod', c)], writes=[PB(b)], inc=(c == 21))
                        S.op('dve', lambda e, half=half, b=b: e.tensor_tensor(out=tmpo[:, half * 512:(half + 1) * 512], in0=pb[b][:, :], in1=g2bc[:, half * 512:(half + 1) * 512], op=ALU.mult),
                             reads=[PB(b), 'g2bc'], writes=['tmpo'])
                    xb_ = x1t[blk]
                    xbres = 'x1t%d' % blk
                    S.op('pool', lambda e, xb_=xb_: e.tensor_tensor(out=xb_[:], in0=tmpo[:], in1=xb_[:], op=ALU.add), reads=['tmpo', xbres], writes=[xbres])
                    s5 = ss5[blk]
                    s5r = 'ss5_%d' % blk
                    S.op('act', lambda e, xb_=xb_, s5=s5: e.activation(out=junk4[:], in_=xb_[:], func=AF.Square, accum_out=s5[:, 0:1]), reads=[xbres], writes=['junk4', s5r])
                    S.op('act', lambda e, s5=s5: e.activation(out=s5[:, 1:2], in_=s5[:, 0:1], func=AF.Sqrt, scale=1.0 / D, bias=cols[:, 127:128]), reads=[s5r, 'cols'], writes=[s5r])
                    S.op('dve', lambda e, s5=s5: e.reciprocal(out=s5[:, 2:3], in_=s5[:, 1:2]), reads=[s5r], writes=[s5r])
                    S.op('dve', lambda e, xb_=xb_, s5=s5: e.scalar_tensor_tensor(out=tmpo[:], in0=xb_[:], scalar=s5[:, 2:3], in1=fgbc[:], op0=ALU.mult, op1=ALU.mult),
                         reads=[xbres, s5r, 'fgbc'], writes=['tmpo'])
                    S.dma(out_d[r0:r0 + 128, :], tmpo[:], 'outd', reads=['tmpo'], writes=['out_d'])
            S.barrier()
        S.barrier()
        S.emit()
        return nc, dbg_outs


def _pk(w, kchunks):
    K, N = w.shape
    return np.ascontiguousarray(w.reshape(kchunks, 128, N).transpose(1, 0, 2))


def _col(v, nch):
    return np.ascontiguousarray(v.reshape(nch, 128).T)


def prep_shared(inp):
    f = lambda a: np.asarray(a, dtype=np.float32)
    sh = {}
    sh["w_ada"] = _pk(f(inp["w_ada"])[0], 8)
    rows = np.concatenate([f(inp["b_ada"])[0], f(inp["norm1_g"])[0], f(inp["norm2_g"])[0], f(inp["final_g"])]).reshape(1, -1)
    sh["rows"] = np.ascontiguousarray(rows)
    w_in = f(inp["w_in"])[0]
    wp = np.zeros((1024, NP_IN), np.float32)
    wp[:, 0:1920] = w_in[:, 0:1920]
    rope = w_in[:, 1920:1952]
    wp[:, C_RM0 + 64:C_RM0 + 96] = rope
    wp[:, C_RR0 + 64:C_RR0 + 80] = rope[:, 16:32]
    wp[:, C_RR0 + 80:C_RR0 + 96] = rope[:, 0:16]
    wp[:, C_GR0:C_GR0 + 1024] = w_in[:, 1952:2976]
    wp[:, C_GM0:C_GM0 + 1024] = w_in[:, 2976:4000]
    sh["w_in_p"] = _pk(wp, 8)

    def bd(w):
        w = w[0]
        o = np.zeros((128, 10, 128), np.float32)
        for c in range(10):
            o[0:64, c, 0:64] = w[2 * c]
            o[64:128, c, 64:128] = w[2 * c + 1]
        return o
    sh["w_ga"] = bd(f(inp["w_gate_a"]))
    sh["w_gx"] = bd(f(inp["w_gate_x"]))
    wuq = f(inp["w_uq"])[0]
    sh["w_uq_m"] = _pk(wuq, 3)
    wr = np.zeros_like(wuq)
    for h in range(NH):
        wr[:, h * 96 + 64:h * 96 + 80] = wuq[:, h * 96 + 80:h * 96 + 96]
        wr[:, h * 96 + 80:h * 96 + 96] = wuq[:, h * 96 + 64:h * 96 + 80]
    sh["w_uq_r"] = _pk(wr, 3)
    wukv = f(inp["w_ukv"])[0].reshape(256, NH, 128)
    sh["w_k"] = _pk(np.ascontiguousarray(wukv[:, :, 0:64]).reshape(256, 1024), 2)
    sh["w_v"] = _pk(np.ascontiguousarray(wukv[:, :, 64:128]).reshape(256, 1024), 2)
    sh["w_pr"] = _pk(f(inp["w_proj_rnn"])[0], 10)
    sh["w_pm"] = _pk(f(inp["w_proj_mla"])[0], 8)
    sh["w_o"] = _pk(f(inp["w_out"])[0], 8)
    sh["w_up"] = _pk(f(inp["w_up"])[0], 8)
    sh["w_dn"] = _pk(f(inp["w_down"])[0], 22)
    fcw = f(inp["ffn_conv_w"])[0]
    sh["fcw"] = np.ascontiguousarray(fcw.reshape(3, 44, 128).transpose(2, 1, 0)).reshape(128, 132)
    sh["fcb"] = _col(f(inp["ffn_conv_b"])[0], 44)
    sh["ident"] = np.eye(128, dtype=np.float32)
    cols = np.zeros((128, 128), np.float32)
    cw = f(inp["conv_w"])[0]
    cols[:, 0:40] = cw.reshape(4, 10, 128).transpose(2, 1, 0).reshape(128, 40)
    cols[:, 40:50] = _col(f(inp["conv_b"])[0], 10)
    cols[:, 50:60] = _col(f(inp["b_gate_a"])[0], 10)
    cols[:, 60:70] = _col(f(inp["b_gate_x"])[0], 10)
    cols[:, 70:80] = _col(f(inp["lru_param"])[0], 10)
    cols[:, 80:83] = _col(f(inp["q_norm_g"])[0], 3)
    cols[:, 83:85] = _col(f(inp["kv_norm_g"])[0], 2)
    half = 16
    inv_freq = (np.float32(10000.0) ** (-np.arange(half, dtype=np.float32) / np.float32(half))).astype(np.float32)
    cols[64:80, 85] = inv_freq
    cols[80:96, 85] = inv_freq
    cols[:, 126] = 1.0
    cols[:, 127] = EPS
    sh["cols"] = cols
    return sh


def prep_core(inp, b):
    d = {}
    d["x"] = np.ascontiguousarray(np.asarray(inp["x"], dtype=np.float32)[b])
    d["c_col"] = _col(np.asarray(inp["c"], dtype=np.float32)[b], 8)
    d["pos"] = np.ascontiguousarray(np.asarray(inp["positions"], dtype=np.int32)[b].reshape(1, S_LEN))
    return d


def kernel(**inputs):
    nc, _ = build_program()
    sh = prep_shared(inputs)
    in_maps = []
    for b in range(8):
        m = dict(sh)
        m.update(prep_core(inputs, b))
        in_maps.append(m)
    res = run_bass_kernel_spmd(nc, in_maps, core_ids=list(range(8)))
    return np.stack([np.asarray(r["out"]).reshape(S_LEN, D) for r in res.results], axis=0).astype(np.float32)
```
